# Optimizing a Trainium2 kernel written in Bass

```python
import math
import jax
import jax.numpy as jnp
from jax import lax
import numpy as np

D_MODEL = 1024
BATCH = 2
SEQ = 8192
DEPTH = 4

GRID_W = 64
CTX_LEN = 256
N_BRANCH = 4
BR_W = 384

S5_GROUP = 16
S5_GROUPS = BR_W // S5_GROUP
S5_STATE = 64
S5_DT_MIN = 1e-3
S5_DT_MAX = 1e-1

SGU_CHUNK = 128
SGU_HEADS = 6

SSD_HEADS = 6
SSD_HEAD_DIM = BR_W // SSD_HEADS
SSD_GROUPS = 2
SSD_STATE = 64
SSD_CHUNK = 128
SSD_CONV = 3
SSD_GN = SSD_GROUPS * SSD_STATE
SSD_CONV_CH = BR_W + 2 * SSD_GN
SSD_DT_MIN = 1e-3
SSD_DT_MAX = 1e-1

MLA_HEADS = 6
MLA_NOPE = 64
MLA_ROPE = 32
MLA_V = 64
MLA_QK = MLA_NOPE + MLA_ROPE
MLA_Q_LORA = 384
MLA_KV_LORA = 256
ATTN_BLOCK = 128
ROPE_BASE = 10000.0
ROPE_PAIRS_AXIS = MLA_ROPE // 4

FFN_HIDDEN = int(math.ceil(8 * D_MODEL / 3 / 256)) * 256

IN_SIZES = (BR_W, 2 * BR_W, BR_W + SSD_CONV_CH + 2 * SSD_HEADS, MLA_Q_LORA + MLA_KV_LORA + MLA_ROPE, N_BRANCH * D_MODEL)
IN_W = sum(IN_SIZES)

kernel_name = 'hybrid_s5_sgu_ssd_mla_prefix_dit'


def rms_norm(x, g, eps=1e-6):
    xf = x.astype(jnp.float32)
    y = xf * lax.rsqrt(jnp.mean(xf * xf, axis=-1, keepdims=True) + eps)
    return (y * g.astype(jnp.float32)).astype(x.dtype)


def layer_norm(x, g, b, eps=1e-5):
    xf = x.astype(jnp.float32)
    mu = jnp.mean(xf, axis=-1, keepdims=True)
    xc = xf - mu
    y = xc * lax.rsqrt(jnp.mean(xc * xc, axis=-1, keepdims=True) + eps)
    return (y * g.astype(jnp.float32) + b.astype(jnp.float32)).astype(x.dtype)


def split_cols(p, sizes):
    idx = [int(i) for i in np.cumsum(sizes)[:-1]]
    return jnp.split(p, idx, axis=-1)


def modulate(h, shift, scale):
    return h * (1.0 + scale) + shift


def s5_discretise(a_re, a_im, b_re, b_im, log_dt):
    f32 = jnp.float32
    a_re, a_im, b_re, b_im = a_re.astype(f32), a_im.astype(f32), b_re.astype(f32), b_im.astype(f32)
    dt = jnp.exp(log_dt.astype(f32))[:, None]
    mag = jnp.exp(a_re * dt)
    ang = a_im * dt
    ab_re = mag * jnp.cos(ang)
    ab_im = mag * jnp.sin(ang)
    den = a_re * a_re + a_im * a_im
    f_re = ((ab_re - 1.0) * a_re + ab_im * a_im) / den
    f_im = (ab_im * a_re - (ab_re - 1.0) * a_im) / den
    bb_re = f_re[..., None] * b_re - f_im[..., None] * b_im
    bb_im = f_re[..., None] * b_im + f_im[..., None] * b_re
    return ab_re, ab_im, bb_re, bb_im


def _affine_combine(e1, e2):
    a1r, a1i, b1r, b1i = e1
    a2r, a2i, b2r, b2i = e2
    return (a2r * a1r - a2i * a1i, a2r * a1i + a2i * a1r,
            a2r * b1r - a2i * b1i + b2r, a2r * b1i + a2i * b1r + b2i)


def s5_scan(ab_re, ab_im, bu_re, bu_im, reverse, h0=None):
    shape = bu_re.shape
    elems = (jnp.broadcast_to(ab_re, shape), jnp.broadcast_to(ab_im, shape), bu_re, bu_im)
    p_re, p_im, h_re, h_im = lax.associative_scan(_affine_combine, elems, axis=1, reverse=reverse)
    if h0 is not None:
        h0r, h0i = h0[0][:, None], h0[1][:, None]
        h_re, h_im = h_re + p_re * h0r - p_im * h0i, h_im + p_re * h0i + p_im * h0r
    return h_re, h_im


def s5_branch(u_lat, u_ctx, lp, need_ctx):
    f32 = jnp.float32

    def drive(u, bb_re, bb_im):
        ug = u.astype(f32).reshape(u.shape[0], u.shape[1], S5_GROUPS, S5_GROUP)
        return jnp.einsum('blgc,gpc->blgp', ug, bb_re), jnp.einsum('blgc,gpc->blgp', ug, bb_im)

    def read(h_re, h_im, c_re, c_im):
        y = jnp.einsum('gcp,blgp->blgc', c_re, h_re) - jnp.einsum('gcp,blgp->blgc', c_im, h_im)
        return y.reshape(y.shape[0], y.shape[1], BR_W)

    d_skip = lp['s5_d'].astype(f32)
    y_lat = d_skip * u_lat.astype(f32)
    y_ctx = d_skip * u_ctx.astype(f32) if need_ctx else None
    for d, rev in enumerate((False, True)):
        ab_re, ab_im, bb_re, bb_im = s5_discretise(lp['s5_a_re'][d], lp['s5_a_im'][d], lp['s5_b_re'][d], lp['s5_b_im'][d], lp['s5_log_dt'][d])
        c_re = lp['s5_c_re'][d].astype(f32)
        c_im = lp['s5_c_im'][d].astype(f32)
        h_re, h_im = s5_scan(ab_re, ab_im, *drive(u_ctx, bb_re, bb_im), reverse=rev)
        end = 0 if rev else -1
        h_last = (h_re[:, end], h_im[:, end])
        if need_ctx:
            y_ctx = y_ctx + read(h_re, h_im, c_re, c_im)
        h_re, h_im = s5_scan(ab_re, ab_im, *drive(u_lat, bb_re, bb_im), reverse=rev, h0=h_last)
        y_lat = y_lat + read(h_re, h_im, c_re, c_im)

    def glu(y, dtype):
        y = jax.nn.gelu(y.astype(dtype))
        return y * jax.nn.sigmoid(y @ lp['s5_w_glu'])

    return glu(y_lat, u_lat.dtype), (glu(y_ctx, u_ctx.dtype) if need_ctx else None)


def sgu_branch(z, lp):
    z = jax.nn.gelu(z)
    u, v = jnp.split(z, 2, axis=-1)
    v = layer_norm(v, lp['sgu_ln_g'], lp['sgu_ln_b'])
    bsz, L, _ = v.shape
    vc = v.reshape(bsz, L // SGU_CHUNK, SGU_CHUNK, SGU_HEADS, BR_W // SGU_HEADS)
    mixed = jnp.einsum('hts,bnshd->bnthd', lp['sgu_w_s'], vc) + lp['sgu_b_s'].T[:, :, None]
    return u * mixed.reshape(bsz, L, BR_W)


def conv_centred(x, w, b):
    y = lax.conv_general_dilated(x, w[:, None, :].astype(x.dtype), window_strides=(1,),
                                 padding=[((SSD_CONV - 1) // 2, SSD_CONV // 2)],
                                 dimension_numbers=('NWC', 'WIO', 'NWC'),
                                 feature_group_count=x.shape[-1])
    return y + b


def ssd_scan(x, dA, Bh, Ch, h0, want_y):
    b, L, H, P = x.shape
    N = Bh.shape[-1]
    T = SSD_CHUNK
    nc = L // T
    xc = x.reshape(b, nc, T, H, P)
    Bc = Bh.reshape(b, nc, T, H, N)
    Cc = Ch.reshape(b, nc, T, H, N)
    a_cum = jnp.cumsum(dA.reshape(b, nc, T, H), axis=2)
    a_last = a_cum[:, :, -1]
    states = jnp.einsum('bclhn,bclh,bclhp->bchpn', Bc, jnp.exp(a_last[:, :, None] - a_cum), xc)

    def step(s, inp):
        st, dec = inp
        return s * dec[..., None, None] + st, s

    final, s_in = lax.scan(step, h0, (jnp.moveaxis(states, 1, 0), jnp.moveaxis(jnp.exp(a_last), 1, 0)))
    if not want_y:
        return None, final
    s_in = jnp.moveaxis(s_in, 0, 1)
    seg = a_cum[:, :, :, None, :] - a_cum[:, :, None, :, :]
    lower = jnp.tril(jnp.ones((T, T), dtype=bool))[None, None, :, :, None]
    decay = jnp.exp(jnp.where(lower, seg, -jnp.inf))
    scores = jnp.einsum('bclhn,bcshn->bclsh', Cc, Bc) * decay
    y_diag = jnp.einsum('bclsh,bcshp->bclhp', scores, xc)
    y_off = jnp.einsum('bclhn,bchpn->bclhp', Cc, s_in) * jnp.exp(a_cum)[..., None]
    return (y_diag + y_off).reshape(b, L, H, P), final


def ssd_branch(p_lat, p_ctx, lp, need_ctx):
    f32 = jnp.float32
    rep = SSD_HEADS // SSD_GROUPS

    def prep(p):
        bsz, L, _ = p.shape
        z, xbc, dt_raw = split_cols(p, (BR_W, SSD_CONV_CH, 2 * SSD_HEADS))
        xbc = jax.nn.silu(conv_centred(xbc, lp['ssd_conv_w'], lp['ssd_conv_b']))
        xs, bm, cm = split_cols(xbc, (BR_W, SSD_GN, SSD_GN))
        xs = xs.reshape(bsz, L, SSD_HEADS, SSD_HEAD_DIM).astype(f32)
        bm = jnp.repeat(bm.reshape(bsz, L, SSD_GROUPS, SSD_STATE), rep, axis=2).astype(f32)
        cm = jnp.repeat(cm.reshape(bsz, L, SSD_GROUPS, SSD_STATE), rep, axis=2).astype(f32)
        return z, xs, bm, cm, dt_raw.astype(f32)

    z_l, x_l, b_l, c_l, dt_l = prep(p_lat)
    z_c, x_c, b_c, c_c, dt_c = prep(p_ctx)
    d_skip = lp['ssd_d'].astype(f32)[:, None]
    y_lat = d_skip * x_l
    y_ctx = d_skip * x_c if need_ctx else None
    for d in range(2):
        A = -jnp.exp(lp['ssd_a_log'][d].astype(f32))
        dt_bias = lp['ssd_dt_bias'][d].astype(f32)

        def run(xs, bm, cm, dt_raw, h0, want_y):
            dt = jax.nn.softplus(dt_raw[..., d * SSD_HEADS:(d + 1) * SSD_HEADS] + dt_bias)
            args = (xs * dt[..., None], dt * A, bm, cm)
            if d == 1:
                args = tuple(jnp.flip(t, axis=1) for t in args)
            y, fin = ssd_scan(*args, h0, want_y)
            if d == 1 and y is not None:
                y = jnp.flip(y, axis=1)
            return y, fin

        h0 = jnp.zeros((x_c.shape[0], SSD_HEADS, SSD_HEAD_DIM, SSD_STATE), f32)
        yc, h_ctx_final = run(x_c, b_c, c_c, dt_c, h0, need_ctx)
        if need_ctx:
            y_ctx = y_ctx + yc
        yl, _ = run(x_l, b_l, c_l, dt_l, h_ctx_final, True)
        y_lat = y_lat + yl

    def out(y, z):
        y = y.reshape(z.shape).astype(z.dtype)
        return rms_norm(y * jax.nn.silu(z), lp['ssd_norm_g'])

    return out(y_lat, z_l), (out(y_ctx, z_c) if need_ctx else None)


def mla_qkv(p, lp):
    bsz, L, _ = p.shape
    cq, ckv, k_rope = split_cols(p, (MLA_Q_LORA, MLA_KV_LORA, MLA_ROPE))
    q = (rms_norm(cq, lp['mla_q_a_norm']) @ lp['mla_w_uq']).reshape(bsz, L, MLA_HEADS, MLA_QK)
    kv = (rms_norm(ckv, lp['mla_kv_a_norm']) @ lp['mla_w_ukv']).reshape(bsz, L, MLA_HEADS, MLA_NOPE + MLA_V)
    k_nope, v = jnp.split(kv, [MLA_NOPE], axis=-1)
    k = jnp.concatenate([k_nope, jnp.broadcast_to(k_rope[:, :, None, :], (bsz, L, MLA_HEADS, MLA_ROPE))], axis=-1)
    return rms_norm(q, lp['mla_q_norm']), rms_norm(k, lp['mla_k_norm']), v


def rope_2d(t, cos, sin):
    nope, rot = t[..., :MLA_NOPE], t[..., MLA_NOPE:]
    x1, x2 = jnp.split(rot, 2, axis=-1)
    c = cos[None, :, None, :].astype(t.dtype)
    s = sin[None, :, None, :].astype(t.dtype)
    return jnp.concatenate([nope, x1 * c - x2 * s, x1 * s + x2 * c], axis=-1)


def softmax_attend(q, k, v):
    s = jnp.einsum('bqhd,bkhd->bhqk', q, k).astype(jnp.float32) * (MLA_QK ** -0.5)
    pr = jax.nn.softmax(s, axis=-1).astype(v.dtype)
    return jnp.einsum('bhqk,bkhd->bqhd', pr, v)


def mla_branch(p_lat, p_ctx, cos, sin, lp, need_ctx):
    q_l, k_l, v_l = mla_qkv(p_lat, lp)
    q_c, k_c, v_c = mla_qkv(p_ctx, lp)
    q_l = rope_2d(q_l, cos, sin)
    k_l = rope_2d(k_l, cos, sin)
    k_all = jnp.concatenate([k_c, k_l], axis=1)
    v_all = jnp.concatenate([v_c, v_l], axis=1)
    bsz, L = q_l.shape[0], q_l.shape[1]
    nb = L // ATTN_BLOCK
    qb = jnp.moveaxis(q_l.reshape(bsz, nb, ATTN_BLOCK, MLA_HEADS, MLA_QK), 1, 0)
    ob = lax.map(lambda qq: softmax_attend(qq, k_all, v_all), qb)
    o_lat = jnp.moveaxis(ob, 0, 1).reshape(bsz, L, MLA_HEADS * MLA_V)
    o_ctx = softmax_attend(q_c, k_c, v_c).reshape(bsz, q_c.shape[1], MLA_HEADS * MLA_V) if need_ctx else None
    return o_lat, o_ctx


def gated_merge(ys, gate_cols, w_branch, w_out):
    gates = jax.nn.sigmoid(gate_cols)
    merged = None
    for i, y in enumerate(ys):
        term = gates[..., i * D_MODEL:(i + 1) * D_MODEL] * (y @ w_branch[i])
        merged = term if merged is None else merged + term
    return merged @ w_out


def token_mixing(h_lat, h_ctx, cos, sin, lp, need_ctx):
    p_lat = h_lat @ lp['w_in']
    p_ctx = h_ctx @ lp['w_in']
    s5_l, sgu_l, ssd_l, mla_l, gate_l = split_cols(p_lat, IN_SIZES)
    s5_c, sgu_c, ssd_c, mla_c, gate_c = split_cols(p_ctx, IN_SIZES)
    a_l, a_c = s5_branch(s5_l, s5_c, lp, need_ctx)
    b_l = sgu_branch(sgu_l, lp)
    b_c = sgu_branch(sgu_c, lp) if need_ctx else None
    c_l, c_c = ssd_branch(ssd_l, ssd_c, lp, need_ctx)
    d_l, d_c = mla_branch(mla_l, mla_c, cos, sin, lp, need_ctx)
    out_l = gated_merge((a_l, b_l, c_l, d_l), gate_l, lp['w_branch'], lp['w_out'])
    out_c = gated_merge((a_c, b_c, c_c, d_c), gate_c, lp['w_branch'], lp['w_out']) if need_ctx else None
    return out_l, out_c


def swiglu(h, w_in, w_out):
    g, u = jnp.split(h @ w_in, 2, axis=-1)
    return (jax.nn.silu(g) * u) @ w_out


def setup_inputs(seed: int = 0) -> dict:
    key = jax.random.key(seed)
    ks = iter(jax.random.split(key, 48))
    f32 = jnp.float32

    def nrm(shape, scale):
        return jax.random.normal(next(ks), shape, f32) * scale

    def gain(shape):
        return 1.0 + nrm(shape, 0.05)

    def log_uniform(shape, lo, hi):
        return jax.random.uniform(next(ks), shape, f32, math.log(lo), math.log(hi))

    x = nrm((BATCH, SEQ, D_MODEL), 1.0)
    c = nrm((BATCH, D_MODEL), 1.0)
    ctx = nrm((BATCH, CTX_LEN, D_MODEL), 1.0)
    c_ctx = nrm((D_MODEL,), 1.0)
    w_ada = nrm((DEPTH, D_MODEL, 6 * D_MODEL), 0.5 * D_MODEL ** -0.5)
    b_ada = nrm((DEPTH, 6 * D_MODEL), 0.02)
    norm1_g = gain((DEPTH, D_MODEL))
    norm2_g = gain((DEPTH, D_MODEL))
    w_in = nrm((DEPTH, D_MODEL, IN_W), D_MODEL ** -0.5)
    sg = (DEPTH, 2, S5_GROUPS, S5_STATE)
    s5_a_re = -0.5 + nrm(sg, 0.01)
    s5_a_im = math.pi * jnp.arange(S5_STATE, dtype=f32) + nrm(sg, 0.01)
    s5_b_re = nrm(sg + (S5_GROUP,), (2 * S5_GROUP) ** -0.5)
    s5_b_im = nrm(sg + (S5_GROUP,), (2 * S5_GROUP) ** -0.5)
    s5_c_re = nrm((DEPTH, 2, S5_GROUPS, S5_GROUP, S5_STATE), S5_STATE ** -0.5)
    s5_c_im = nrm((DEPTH, 2, S5_GROUPS, S5_GROUP, S5_STATE), S5_STATE ** -0.5)
    s5_log_dt = log_uniform((DEPTH, 2, S5_GROUPS), S5_DT_MIN, S5_DT_MAX)
    s5_d = nrm((DEPTH, BR_W), 1.0)
    s5_w_glu = nrm((DEPTH, BR_W, BR_W), BR_W ** -0.5)
    sgu_ln_g = gain((DEPTH, BR_W))
    sgu_ln_b = nrm((DEPTH, BR_W), 0.02)
    sgu_w_s = nrm((DEPTH, SGU_HEADS, SGU_CHUNK, SGU_CHUNK), SGU_CHUNK ** -0.5)
    sgu_b_s = 1.0 + nrm((DEPTH, SGU_HEADS, SGU_CHUNK), 0.1)
    ssd_conv_w = nrm((DEPTH, SSD_CONV, SSD_CONV_CH), SSD_CONV ** -0.5)
    ssd_conv_b = nrm((DEPTH, SSD_CONV_CH), 0.02)
    ssd_a_log = jnp.log(jax.random.uniform(next(ks), (DEPTH, 2, SSD_HEADS), f32, 1.0, 16.0))
    dt0 = jnp.exp(log_uniform((DEPTH, 2, SSD_HEADS), SSD_DT_MIN, SSD_DT_MAX))
    ssd_dt_bias = dt0 + jnp.log(-jnp.expm1(-dt0))
    ssd_d = 1.0 + nrm((DEPTH, SSD_HEADS), 0.1)
    ssd_norm_g = gain((DEPTH, BR_W))
    mla_q_a_norm = gain((DEPTH, MLA_Q_LORA))
    mla_w_uq = nrm((DEPTH, MLA_Q_LORA, MLA_HEADS * MLA_QK), MLA_Q_LORA ** -0.5)
    mla_kv_a_norm = gain((DEPTH, MLA_KV_LORA))
    mla_w_ukv = nrm((DEPTH, MLA_KV_LORA, MLA_HEADS * (MLA_NOPE + MLA_V)), MLA_KV_LORA ** -0.5)
    mla_q_norm = gain((DEPTH, MLA_QK))
    mla_k_norm = gain((DEPTH, MLA_QK))
    w_branch = nrm((DEPTH, N_BRANCH, BR_W, D_MODEL), BR_W ** -0.5)
    w_out = nrm((DEPTH, D_MODEL, D_MODEL), D_MODEL ** -0.5)
    w_ffn_in = nrm((DEPTH, D_MODEL, 2 * FFN_HIDDEN), D_MODEL ** -0.5)
    w_ffn_out = nrm((DEPTH, FFN_HIDDEN, D_MODEL), FFN_HIDDEN ** -0.5)
    return {'x': x, 'c': c, 'ctx': ctx, 'c_ctx': c_ctx, 'w_ada': w_ada, 'b_ada': b_ada,
            'norm1_g': norm1_g, 'norm2_g': norm2_g, 'w_in': w_in,
            's5_a_re': s5_a_re, 's5_a_im': s5_a_im, 's5_b_re': s5_b_re, 's5_b_im': s5_b_im,
            's5_c_re': s5_c_re, 's5_c_im': s5_c_im, 's5_log_dt': s5_log_dt, 's5_d': s5_d, 's5_w_glu': s5_w_glu,
            'sgu_ln_g': sgu_ln_g, 'sgu_ln_b': sgu_ln_b, 'sgu_w_s': sgu_w_s, 'sgu_b_s': sgu_b_s,
            'ssd_conv_w': ssd_conv_w, 'ssd_conv_b': ssd_conv_b, 'ssd_a_log': ssd_a_log,
            'ssd_dt_bias': ssd_dt_bias, 'ssd_d': ssd_d, 'ssd_norm_g': ssd_norm_g,
            'mla_q_a_norm': mla_q_a_norm, 'mla_w_uq': mla_w_uq, 'mla_kv_a_norm': mla_kv_a_norm,
            'mla_w_ukv': mla_w_ukv, 'mla_q_norm': mla_q_norm, 'mla_k_norm': mla_k_norm,
            'w_branch': w_branch, 'w_out': w_out, 'w_ffn_in': w_ffn_in, 'w_ffn_out': w_ffn_out}


def reference(x, c, ctx, c_ctx, w_ada, b_ada, norm1_g, norm2_g, w_in,
              s5_a_re, s5_a_im, s5_b_re, s5_b_im, s5_c_re, s5_c_im, s5_log_dt, s5_d, s5_w_glu,
              sgu_ln_g, sgu_ln_b, sgu_w_s, sgu_b_s,
              ssd_conv_w, ssd_conv_b, ssd_a_log, ssd_dt_bias, ssd_d, ssd_norm_g,
              mla_q_a_norm, mla_w_uq, mla_kv_a_norm, mla_w_ukv, mla_q_norm, mla_k_norm,
              w_branch, w_out, w_ffn_in, w_ffn_out):
    f32 = jnp.float32
    L = x.shape[1]
    n_rows = L // GRID_W
    pos_row = jnp.repeat(jnp.arange(n_rows, dtype=f32), GRID_W)
    pos_col = jnp.tile(jnp.arange(GRID_W, dtype=f32), n_rows)
    inv_freq = ROPE_BASE ** (-jnp.arange(ROPE_PAIRS_AXIS, dtype=f32) / ROPE_PAIRS_AXIS)
    ang = jnp.concatenate([pos_row[:, None] * inv_freq, pos_col[:, None] * inv_freq], axis=-1)
    cos, sin = jnp.cos(ang), jnp.sin(ang)

    silu_c = jax.nn.silu(c)
    silu_cc = jax.nn.silu(c_ctx)
    h_ctx_stream = ctx
    for l in range(DEPTH):
        need_ctx = l < DEPTH - 1
        lp = {'w_in': w_in[l],
              's5_a_re': s5_a_re[l], 's5_a_im': s5_a_im[l], 's5_b_re': s5_b_re[l], 's5_b_im': s5_b_im[l],
              's5_c_re': s5_c_re[l], 's5_c_im': s5_c_im[l], 's5_log_dt': s5_log_dt[l], 's5_d': s5_d[l],
              's5_w_glu': s5_w_glu[l],
              'sgu_ln_g': sgu_ln_g[l], 'sgu_ln_b': sgu_ln_b[l], 'sgu_w_s': sgu_w_s[l], 'sgu_b_s': sgu_b_s[l],
              'ssd_conv_w': ssd_conv_w[l], 'ssd_conv_b': ssd_conv_b[l], 'ssd_a_log': ssd_a_log[l],
              'ssd_dt_bias': ssd_dt_bias[l], 'ssd_d': ssd_d[l], 'ssd_norm_g': ssd_norm_g[l],
              'mla_q_a_norm': mla_q_a_norm[l], 'mla_w_uq': mla_w_uq[l], 'mla_kv_a_norm': mla_kv_a_norm[l],
              'mla_w_ukv': mla_w_ukv[l], 'mla_q_norm': mla_q_norm[l], 'mla_k_norm': mla_k_norm[l],
              'w_branch': w_branch[l], 'w_out': w_out[l]}
        mod_l = (silu_c @ w_ada[l] + b_ada[l])[:, None, :]
        mod_c = silu_cc @ w_ada[l] + b_ada[l]
        sh1, sc1, g1, sh2, sc2, g2 = jnp.split(mod_l, 6, axis=-1)
        csh1, csc1, cg1, csh2, csc2, cg2 = jnp.split(mod_c, 6, axis=-1)
        h_lat = modulate(rms_norm(x, norm1_g[l]), sh1, sc1)
        h_ctx = modulate(rms_norm(h_ctx_stream, norm1_g[l]), csh1, csc1)
        mix_lat, mix_ctx = token_mixing(h_lat, h_ctx, cos, sin, lp, need_ctx)
        x = x + g1 * mix_lat
        x = x + g2 * swiglu(modulate(rms_norm(x, norm2_g[l]), sh2, sc2), w_ffn_in[l], w_ffn_out[l])
        if need_ctx:
            h_ctx_stream = h_ctx_stream + cg1 * mix_ctx
            h_ctx_stream = h_ctx_stream + cg2 * swiglu(modulate(rms_norm(h_ctx_stream, norm2_g[l]), csh2, csc2), w_ffn_in[l], w_ffn_out[l])
    return x
```

```python
import math
from contextlib import ExitStack
import numpy as np
import concourse.bass as bass
import concourse.mybir as mybir
from concourse.bass_utils import run_bass_kernel_spmd

F32 = mybir.dt.float32
BF16 = mybir.dt.bfloat16
AF = mybir.ActivationFunctionType
ALU = mybir.AluOpType

D = 1024
KC = 8
CTX = 256
BRW = 384
INW = 6956
FFH = 2816
NSLOT = 16
PI = math.pi

O_S5 = 0
O_SGU = 384
O_Z = 1152
O_XBC = 1536
O_DT = 2176
O_CQ = 2188
O_CKV = 2572
O_ROPE = 2828
O_GATE = 2860


class Dep:
    __slots__ = ("w", "r")

    def __init__(self):
        self.w = None
        self.r = {}


class Tl:
    def __init__(self, t, n=1):
        self.t = t
        self.d = [Dep() for _ in range(n)]

    def __getitem__(self, idx):
        return self.t[idx]


class KB:
    def __init__(self, nc, es):
        self.nc = nc
        self.eng = {}
        for name, h in (("pe", nc.tensor), ("act", nc.scalar), ("dve", nc.vector),
                        ("pool", nc.gpsimd), ("sp", nc.sync)):
            sem = es.enter_context(nc.semaphore("sem_" + name))
            self.eng[name] = dict(h=h, sem=sem, cnt=0, seen={}, name=name)
        self.dq = {}
        for q in ("sp", "pool"):
            slots = [dict(sem=es.enter_context(nc.semaphore(f"dq_{q}_{i}")), tot=0) for i in range(NSLOT)]
            self.dq[q] = dict(slots=slots, nxt=0)
        self.ninst = 0

    def _sem(self, k):
        if k[0] == "e":
            return self.eng[k[1]]["sem"]
        return self.dq[k[1]]["slots"][k[2]]["sem"]

    def _collect(self, reads, writes):
        need = {}
        for d in reads:
            if d.w is not None and need.get(d.w[0], 0) < d.w[1]:
                need[d.w[0]] = d.w[1]
        for d in writes:
            if d.w is not None and need.get(d.w[0], 0) < d.w[1]:
                need[d.w[0]] = d.w[1]
            for k, v in d.r.items():
                if need.get(k, 0) < v:
                    need[k] = v
        return need

    def _wait(self, E, need):
        for k, v in need.items():
            if E["name"] == "pe" and k == ("e", "pe"):
                continue
            if E["seen"].get(k, 0) >= v:
                continue
            E["h"].wait_ge(self._sem(k), v)
            E["seen"][k] = v

    def op(self, eng, fn, reads=(), writes=()):
        E = self.eng[eng]
        self._wait(E, self._collect(reads, writes))
        ins = fn(E["h"])
        E["cnt"] += 1
        ins.then_inc(E["sem"], 1)
        key = ("e", eng)
        c = E["cnt"]
        for d in writes:
            d.w = (key, c)
            d.r = {}
        for d in reads:
            if d.r.get(key, 0) < c:
                d.r[key] = c
        self.ninst += 1
        return ins

    def dma(self, q, out, in_, reads=(), writes=(), **kw):
        E = self.eng[q]
        Q = self.dq[q]
        i = Q["nxt"]
        Q["nxt"] = (i + 1) % NSLOT
        sl = Q["slots"][i]
        need = self._collect(reads, writes)
        k = ("d", q, i)
        if sl["tot"] > 0 and need.get(k, 0) < sl["tot"]:
            need[k] = sl["tot"]
        self._wait(E, need)
        ins = E["h"].dma_start(out=out, in_=in_, **kw)
        sl["tot"] += 16
        ins.then_inc(sl["sem"], 16)
        for d in writes:
            d.w = (k, sl["tot"])
            d.r = {}
        for d in reads:
            if d.r.get(k, 0) < sl["tot"]:
                d.r[k] = sl["tot"]
        self.ninst += 1

    def barrier(self, engines=("pe", "act", "dve", "pool", "sp")):
        need = {}
        for n, E in self.eng.items():
            if E["cnt"] > 0:
                need[("e", n)] = E["cnt"]
        for q, Q in self.dq.items():
            for i, sl in enumerate(Q["slots"]):
                if sl["tot"] > 0:
                    need[("d", q, i)] = sl["tot"]
        for n in engines:
            E = self.eng[n]
            nd = {k: v for k, v in need.items() if k != ("e", n)}
            self._wait(E, nd)

    def mm(self, out, lhsT, rhs, start, stop, reads, writes):
        return self.op("pe", lambda e: e.matmul(out, lhsT, rhs, start=start, stop=stop), reads, writes)

    def tr(self, out, in_, ident, reads, writes):
        return self.op("pe", lambda e: e.transpose(out, in_, ident), reads, writes)

    def act(self, out, in_, func, reads, writes, scale=None, bias=None):
        kw = {}
        if scale is not None:
            kw["scale"] = scale
        if bias is not None:
            kw["bias"] = bias
        return self.op("act", lambda e: e.activation(out, in_, func, **kw), reads, writes)

    def tt(self, out, in0, in1, op, reads, writes, eng="dve"):
        return self.op(eng, lambda e: e.tensor_tensor(out, in0, in1, op), reads, writes)

    def ts(self, out, in0, s1, op0, reads, writes, s2=None, op1=None, eng="dve"):
        if op1 is None:
            return self.op(eng, lambda e: e.tensor_scalar(out, in0, s1, None, op0), reads, writes)
        return self.op(eng, lambda e: e.tensor_scalar(out, in0, s1, s2, op0, op1), reads, writes)

    def stt(self, out, in0, scalar, in1, op0, op1, reads, writes):
        return self.op("dve", lambda e: e.scalar_tensor_tensor(out, in0, scalar, in1, op0, op1), reads, writes)

    def cp(self, out, in_, reads, writes, eng="dve"):
        if eng == "act":
            return self.op("act", lambda e: e.copy(out, in_), reads, writes)
        return self.op(eng, lambda e: e.tensor_copy(out, in_), reads, writes)

    def recip(self, out, in_, reads, writes):
        return self.op("dve", lambda e: e.reciprocal(out, in_), reads, writes)

    def memset(self, ap, val, writes, eng="dve"):
        return self.op(eng, lambda e: e.memset(ap, val), (), writes)


class Cfg:
    def __init__(self, L=8192, depth=4, debug=False, stop_after=None):
        self.L = L
        self.depth = depth
        self.NT = CTX + L
        self.tiles = [(0, 256, True)] + [(CTX + 512 * i, 512, False) for i in range(L // 512)]
        self.debug = debug
        self.stop_after = stop_after
        self.NB = self.NT // 128


def _chunk_cols(v, n):
    v = np.asarray(v, np.float32)
    return np.ascontiguousarray(np.swapaxes(v.reshape(v.shape[:-1] + (n, 128)), -1, -2))


def _consts(L):
    c = np.zeros((128, 12, 128), np.float32)
    j = np.arange(128)
    c[:, 0, :] = 1.0
    c[:, 1, :] = np.eye(128)
    sw = np.zeros((128, 128), np.float32)
    for i in range(16):
        sw[80 + i, 64 + i] = -1.0
        sw[64 + i, 80 + i] = 1.0
    c[:, 2, :] = sw
    sr = np.zeros((128, 128), np.float32)
    for i in range(32):
        sr[i, 64 + i] = 1.0
    c[:, 3, :] = sr
    c[:, 4, :] = (j[:, None] > j[None, :])
    c[:, 5, :] = (j[:, None] <= j[None, :])
    c[:, 6, :] = (j[:, None] < j[None, :])
    c[:, 7, :] = (j[:, None] >= j[None, :])
    return c


def _rope_tables(L):
    n_rows = L // 64
    pos_row = np.repeat(np.arange(n_rows, dtype=np.float32), 64)
    pos_col = np.tile(np.arange(64, dtype=np.float32), n_rows)
    inv_freq = (10000.0 ** (-np.arange(8, dtype=np.float32) / 8)).astype(np.float32)
    ang = np.concatenate([pos_row[:, None] * inv_freq, pos_col[:, None] * inv_freq], axis=-1).astype(np.float32)
    cs = np.zeros((2, 96, L), np.float32)
    cs[0, :64, :] = 1.0
    cs[0, 64:80, :] = np.cos(ang).T
    cs[0, 80:96, :] = np.cos(ang).T
    cs[1, 64:80, :] = np.sin(ang).T
    cs[1, 80:96, :] = np.sin(ang).T
    return cs


def prep_shared(inp, cfg):
    dp = cfg.depth
    f = lambda k: np.asarray(inp[k], np.float32)[:dp]
    o = {}
    for k in ("w_ada", "w_in", "w_ffn_in", "w_ffn_out", "w_branch", "w_out", "s5_w_glu",
              "sgu_w_s", "sgu_b_s", "mla_w_uq", "mla_w_ukv"):
        o[k] = np.ascontiguousarray(f(k))
    o["b_ada_l"] = _chunk_cols(f("b_ada"), 48)
    o["n1g"] = _chunk_cols(f("norm1_g"), 8)
    o["n2g"] = _chunk_cols(f("norm2_g"), 8)
    a_re, a_im, ldt = f("s5_a_re"), f("s5_a_im"), f("s5_log_dt")
    b_re, b_im, c_re, c_im = f("s5_b_re"), f("s5_b_im"), f("s5_c_re"), f("s5_c_im")
    rep = np.zeros((dp, 5, 128, 24, 128), np.float32)
    cpl = np.zeros((dp, 2, 128, 24, 128), np.float32)
    col = np.zeros((dp, 3, 128, 24), np.float32)
    for d in range(2):
        for q in range(12):
            st = d * 12 + q
            for gg in range(2):
                g = 2 * q + gg
                ps = slice(gg * 64, gg * 64 + 64)
                rep[:, 0, :, st, ps] = a_re[:, d, g, None, :]
                rep[:, 1, :, st, ps] = a_im[:, d, g, None, :]
                rep[:, 2, :, st, ps] = ldt[:, d, g, None, None]
                r0 = 32 * (q % 4) + gg * 16
                rep[:, 3, r0:r0 + 16, st, ps] = np.swapaxes(b_re[:, d, g], -1, -2)
                rep[:, 4, r0:r0 + 16, st, ps] = np.swapaxes(b_im[:, d, g], -1, -2)
                cpl[:, 0, ps, st, r0:r0 + 16] = np.swapaxes(c_re[:, d, g], -1, -2)
                cpl[:, 1, ps, st, r0:r0 + 16] = np.swapaxes(c_im[:, d, g], -1, -2)
                col[:, 0, ps, st] = a_re[:, d, g, :]
                col[:, 1, ps, st] = a_im[:, d, g, :]
                col[:, 2, ps, st] = ldt[:, d, g, None]
    o["s5_rep"] = rep
    o["s5_cpl"] = cpl
    o["s5_col"] = col
    o["s5_dcol"] = _chunk_cols(f("s5_d"), 3)
    o["sgu_ln"] = np.concatenate([_chunk_cols(f("sgu_ln_g"), 3), _chunk_cols(f("sgu_ln_b"), 3)], axis=-1)
    cw = f("ssd_conv_w")
    cb = f("ssd_conv_b")
    o["ssd_cw"] = np.ascontiguousarray(np.stack(
        [_chunk_cols(cw[:, 0], 5), _chunk_cols(cw[:, 1], 5), _chunk_cols(cw[:, 2], 5), _chunk_cols(cb, 5)], axis=-1))
    srep = np.zeros((dp, 128, 30), np.float32)
    srep[:, :, 0:12] = f("ssd_a_log").reshape(dp, 1, 12)
    srep[:, :, 12:24] = f("ssd_dt_bias").reshape(dp, 1, 12)
    srep[:, :, 24:30] = f("ssd_d").reshape(dp, 1, 6)
    o["ssd_rep"] = srep
    ng = f("ssd_norm_g").reshape(dp, 6, 64)
    o["ssd_ng"] = np.ascontiguousarray(np.swapaxes(ng, -1, -2))
    mc = np.zeros((dp, 128, 16), np.float32)
    mc[:, :, 0:3] = _chunk_cols(f("mla_q_a_norm"), 3)
    mc[:, :, 3:5] = _chunk_cols(f("mla_kv_a_norm"), 2)
    for ci, key in ((5, "mla_q_norm"), (7, "mla_k_norm")):
        g = f(key)
        mc[:, :96, ci] = g
        mc[:, 64:80, ci + 1] = g[:, 80:96]
        mc[:, 80:96, ci + 1] = g[:, 64:80]
    o["mla_cols"] = mc
    o["cst"] = _consts(cfg.L)
    o["rope_cs"] = _rope_tables(cfg.L)
    return o


def prep_core(inp, b, cfg):
    x = np.asarray(inp["x"], np.float32)[b, :cfg.L]
    ctx = np.asarray(inp["ctx"], np.float32)[b]
    xin = np.ascontiguousarray(np.concatenate([ctx, x], axis=0).T)
    cc = np.stack([_chunk_cols(np.asarray(inp["c"], np.float32)[b], 8),
                   _chunk_cols(np.asarray(inp["c_ctx"], np.float32), 8)], axis=-1)
    return {"xin": xin, "cc": np.ascontiguousarray(cc)}


class Prog:
    def __init__(self, cfg):
        self.cfg = cfg
        self.nc = bass.Bass("TRN2", target_bir_lowering=False)
        self.es = ExitStack()
        self.kb = KB(self.nc, self.es)
        self.uid = 0
        self.dr = {}
        self.outs = []

    def sb(self, ph, name, shape, dt, n=1):
        self.uid += 1
        return Tl(ph.enter_context(self.nc.sbuf_tensor(f"{name}_{self.uid}", shape, dt)), n)

    def ps(self, ph, name, shape, dt=F32, n=1):
        self.uid += 1
        return Tl(ph.enter_context(self.nc.psum_tensor(f"{name}_{self.uid}", shape, dt)), n)

    def din(self, name, shape, dt=F32):
        t = self.nc.dram_tensor(name, list(shape), dt, kind="ExternalInput")
        self.dr[name] = Tl(t.ap(), 1)
        return self.dr[name]

    def dscr(self, name, shape, dt, n=1, out=False):
        kind = "ExternalOutput" if (out or self.cfg.debug) else "Internal"
        t = self.nc.dram_tensor(name, list(shape), dt, kind=kind)
        self.dr[name] = Tl(t.ap(), n)
        if kind == "ExternalOutput":
            self.outs.append(name)
        return self.dr[name]

    def declare(self, shapes):
        cfg = self.cfg
        NT, dp = cfg.NT, cfg.depth
        for k, shp in shapes.items():
            self.din(k, shp)
        nt = len(cfg.tiles)
        self.dscr("out", [D, NT], F32, n=nt, out=True)
        self.dscr("WIN", [dp, 128, 8 * INW], BF16, n=dp * 16)
        self.dscr("WFI", [dp, 128, 8 * 2 * FFH], BF16, n=dp)
        self.dscr("WFO", [dp, 128, 22 * D], BF16, n=dp)
        self.dscr("WBR", [dp, 4 * BRW, D], BF16, n=dp)
        self.dscr("WOUT", [dp, D, D], BF16, n=dp)
        self.dscr("US5", [BRW, NT], BF16, n=nt)
        self.dscr("GATES", [4 * D, NT], BF16, n=nt)
        self.dscr("YB", [BRW, NT], BF16, n=nt)
        self.dscr("ZS", [6, 64, NT], BF16, n=nt)
        self.dscr("XBC", [640, NT], BF16, n=nt)
        self.dscr("DTR", [12, NT], F32, n=nt)
        self.dscr("QT", [6, 96, NT], BF16, n=nt)
        self.dscr("KT", [6, 96, NT], BF16, n=nt)
        self.dscr("VE", [cfg.NB, 128, 6 * 128], BF16, n=nt)
        self.dscr("YS5", [2, BRW, NT], F32, n=2 * nt)
        self.dscr("YA", [BRW, NT], BF16, n=nt)
        self.dscr("YC", [6, 64, NT], BF16, n=nt)
        self.dscr("YD", [6, 64, NT], BF16, n=nt)
        self.dscr("TBL", [24, 128, 2, 512], F32, n=24)
        self.dscr("XC", [640, NT], BF16, n=nt)
        self.dscr("SINF", [cfg.NB, 128, 384], BF16, n=cfg.NB)

    def win_slabs(self):
        slabs = [(O_S5, 384, [(128 * i, 128, "s5", i) for i in range(3)]),
                 (O_SGU, 384, [(128 * i, 128, "sgu", i) for i in range(3)]),
                 (O_SGU + 384, 384, [(128 * i, 128, "sgu", 3 + i) for i in range(3)]),
                 (O_Z, 384, [(64 * i, 64, "z", i) for i in range(6)]),
                 (O_XBC, 512, [(128 * i, 128, "xbc", i) for i in range(4)]),
                 (O_XBC + 512, 140, [(0, 128, "xbc", 4), (128, 12, "dt", 0)]),
                 (O_CQ, 384, [(128 * i, 128, "cq", i) for i in range(3)]),
                 (O_CKV, 288, [(0, 128, "ckv", 0), (128, 128, "ckv", 1), (256, 32, "rope", 0)])]
        for g in range(8):
            slabs.append((O_GATE + 512 * g, 512, [(128 * i, 128, "gate", g) for i in range(4)]))
        return slabs

    def cast_weights(self, l):
        kb = self.kb
        D_ = self.dr
        for si_, (c0, ncol, _) in enumerate(self.win_slabs()):
            src = D_["w_in"][l][:, c0:c0 + ncol].rearrange("(k r) c -> r k c", r=128)
            dst = D_["WIN"][l][:, 8 * c0:8 * (c0 + ncol)].rearrange("r (k c) -> r k c", c=ncol)
            kb.dma("pool", dst, src, reads=(), writes=[D_["WIN"].d[l * 16 + si_]])
        for sj in range(22):
            src = D_["w_ffn_in"][l][:, sj * 256:(sj + 1) * 256].rearrange("(k r) c -> r k c", r=128)
            dst = D_["WFI"][l][:, sj * 2048:(sj + 1) * 2048].rearrange("r (k c) -> r k c", c=256)
            kb.dma("pool", dst, src, reads=(), writes=[D_["WFI"].d[l]])
        for m in range(8):
            src = D_["w_ffn_out"][l][:, m * 128:(m + 1) * 128].rearrange("(k r) c -> r k c", r=128)
            dst = D_["WFO"][l][:, m * 2816:(m + 1) * 2816].rearrange("r (k c) -> r k c", c=128)
            kb.dma("pool", dst, src, reads=(), writes=[D_["WFO"].d[l]])
        s_ = D_["w_out"][l].rearrange("r (a b) -> r a b", b=512)
        o_ = D_["WOUT"][l].rearrange("r (a b) -> r a b", b=512)
        for r0 in range(0, D, 256):
            kb.dma("pool", o_[r0:r0 + 256], s_[r0:r0 + 256], reads=(), writes=[D_["WOUT"].d[l]])
        s_ = D_["w_branch"][l].rearrange("i r (a b) -> (i r) a b", b=512)
        o_ = D_["WBR"][l].rearrange("r (a b) -> r a b", b=512)
        for r0 in range(0, 4 * BRW, 256):
            kb.dma("pool", o_[r0:r0 + 256], s_[r0:r0 + 256], reads=(), writes=[D_["WBR"].d[l]])

    def load_consts(self):
        kb, es = self.kb, self.es
        self.cst = self.sb(es, "cst", [128, 12, 128], F32)
        kb.dma("sp", self.cst[:], self.dr["cst"][:], reads=(), writes=self.cst.d)
        self.cstb = self.sb(es, "cstb", [128, 4, 128], BF16)
        kb.cp(self.cstb[:], self.cst[:, 0:4, :], self.cst.d, self.cstb.d)
        self.cc = self.sb(es, "cc", [128, 8, 2], F32)
        kb.dma("sp", self.cc[:], self.dr["cc"][:], reads=(), writes=self.cc.d)
        self.csil = self.sb(es, "csil", [128, 8, 2], F32)
        kb.act(self.csil[:], self.cc[:], AF.Silu, self.cc.d, self.csil.d)
        dp_ = self.cfg.depth
        self.MODs = [self.sb(es, "MOD", [128, 48, 2], F32) for _ in range(dp_)]
        self.S1s = [self.sb(es, "S1", [128, 8, 2], F32) for _ in range(dp_)]
        self.S2s = [self.sb(es, "S2", [128, 8, 2], F32) for _ in range(dp_)]
        self.ng = self.sb(es, "ng", [128, 16], F32)
        self.bada = self.sb(es, "bada", [128, 48], F32)

    @property
    def ones_bf(self):
        return self.cstb[:, 0, :]

    @property
    def ident_bf(self):
        return self.cstb[:, 1, :]

    def set_layer(self, l):
        self.MOD, self.S1, self.S2 = self.MODs[l], self.S1s[l], self.S2s[l]

    def layer_mod(self, l):
        kb = self.kb
        self.set_layer(l)
        D_ = self.dr
        kb.dma("sp", self.ng[:, 0:8], D_["n1g"][l], reads=(), writes=self.ng.d)
        kb.dma("sp", self.ng[:, 8:16], D_["n2g"][l], reads=(), writes=self.ng.d)
        kb.dma("sp", self.bada[:], D_["b_ada_l"][l], reads=(), writes=self.bada.d)
        with ExitStack() as ph:
            wa = [self.sb(ph, "wa", [128, 8, 768], F32) for _ in range(2)]
            pm = self.ps(ph, "pmod", [128, 48, 2], F32)
            wsrc = D_["w_ada"][l].rearrange("(k r) c -> r k c", r=128)
            for blk in range(8):
                w = wa[blk % 2]
                kb.dma("sp", w[:], wsrc[:, :, blk * 768:(blk + 1) * 768], reads=(), writes=w.d)
                for mm in range(6):
                    m = blk * 6 + mm
                    for k in range(8):
                        kb.mm(pm[:, m, :], w[:, k, mm * 128:(mm + 1) * 128], self.csil[:, k, :],
                              k == 0, k == 7, w.d + self.csil.d, pm.d)
            kb.tt(self.MOD[:], pm[:], self.bada[:].unsqueeze(2).broadcast_to([128, 48, 2]), ALU.add,
                  pm.d + self.bada.d, self.MOD.d)
            for S, g0, c0 in ((self.S1, 0, 8), (self.S2, 8, 32)):
                kb.ts(S[:], self.MOD[:, c0:c0 + 8, :], 1.0, ALU.add, self.MOD.d, S.d)
                kb.tt(S[:], S[:], self.ng[:, g0:g0 + 8].unsqueeze(2).broadcast_to([128, 8, 2]), ALU.mult,
                      S.d + self.ng.d, S.d)
            kb.barrier()

    def norm_mod(self, xt, TT, j, S, sh0, sq, h, pst, r1, r2, xn=None):
        kb = self.kb
        xn = xn or xt
        kb.act(sq[:, :, :TT], xt[:, :, :TT], AF.Square, xt.d, sq.d)
        for k in range(8):
            kb.mm(pst[:, :TT], self.ones_bf, sq[:, k, :TT], k == 0, k == 7, sq.d + self.cstb.d, pst.d)
        kb.act(r1[:, :TT], pst[:, :TT], AF.Sqrt, pst.d, r1.d, scale=1.0 / D, bias=1e-6)
        kb.recip(r2[:, :TT], r1[:, :TT], r1.d, r2.d)
        kb.tt(xn[:, :, :TT], xt[:, :, :TT], r2[:, :TT].unsqueeze(1).broadcast_to([128, 8, TT]), ALU.mult,
              xt.d + r2.d, xn.d)
        for k in range(8):
            kb.act(h[:, k, :TT], xn[:, k, :TT], AF.Identity, xn.d + S.d + self.MOD.d, h.d,
                   scale=S[:, k, j:j + 1], bias=self.MOD[:, sh0 + k, j:j + 1])

    def load_layer_a(self, l, ph):
        kb, D_ = self.kb, self.dr
        R = {}
        R["sguln"] = self.sb(ph, "sguln", [128, 6], F32)
        kb.dma("sp", R["sguln"][:], D_["sgu_ln"][l], reads=(), writes=R["sguln"].d)
        R["mcol"] = self.sb(ph, "mcol", [128, 16], F32)
        kb.dma("sp", R["mcol"][:], D_["mla_cols"][l], reads=(), writes=R["mcol"].d)
        R["BS"] = self.sb(ph, "BS", [128, 3, 128], F32)
        for hh in range(6):
            kb.dma("sp", R["BS"][(hh % 2) * 64:(hh % 2) * 64 + 64, hh // 2, :],
                   D_["sgu_b_s"][l, hh, :].partition_broadcast(64), reads=(), writes=R["BS"].d)
        R["WsT"] = self.sb(ph, "WsT", [128, 6, 128], BF16)
        with ExitStack() as p2:
            wn = self.sb(p2, "wsnat", [128, 6, 128], F32)
            kb.dma("sp", wn[:], D_["sgu_w_s"][l].rearrange("h t s -> t h s"), reads=(), writes=wn.d)
            pt = self.ps(p2, "pwst", [128, 6, 128], F32)
            for hh in range(6):
                kb.tr(pt[:, hh, :], wn[:, hh, :], self.cst[:, 1, :], wn.d + self.cst.d, pt.d)
            kb.cp(R["WsT"][:], pt[:], pt.d, R["WsT"].d)
            kb.barrier()
        R["Wuq"] = self.sb(ph, "Wuq", [128, 3, 576], BF16)
        kb.dma("pool", R["Wuq"][:], D_["mla_w_uq"][l].rearrange("(i r) c -> r i c", r=128), reads=(), writes=R["Wuq"].d)
        R["Wk"] = self.sb(ph, "Wk", [128, 2, 6, 96], BF16)
        kb.memset(R["Wk"][:], 0.0, R["Wk"].d)
        wsrc = D_["mla_w_ukv"][l].rearrange("(i r) (h c) -> r i h c", r=128, c=128)
        for i in range(2):
            kb.dma("pool", R["Wk"][:, i, :, 0:64], wsrc[:, i, :, 0:64], reads=(), writes=R["Wk"].d)
        R["Wv"] = self.sb(ph, "Wv", [128, 2, 6, 64], BF16)
        for i in range(2):
            kb.dma("pool", R["Wv"][:, i, :, :], wsrc[:, i, :, 64:128], reads=(), writes=R["Wv"].d)
        return R

    def pass_a(self, l):
        kb, cfg, D_ = self.kb, self.cfg, self.dr
        xsrc = D_["xin"] if l == 0 else D_["out"]
        xv = xsrc.t.rearrange("(k r) t -> r k t", r=128)
        win = D_["WIN"][l]
        with ExitStack() as ph:
            R = self.load_layer_a(l, ph)
            xt = [self.sb(ph, "xt", [128, 8, 512], F32) for _ in range(2)]
            sq = self.sb(ph, "sq", [128, 8, 512], BF16)
            h = self.sb(ph, "h", [128, 8, 512], BF16)
            wsl = [self.sb(ph, "wsl", [128, 8, 512], BF16) for _ in range(3)]
            us = self.sb(ph, "us", [128, 3, 512], BF16)
            zgs = [self.sb(ph, "zg", [128, 6, 512], BF16, n=6) for _ in range(2)]
            zs = self.sb(ph, "zs", [64, 6, 512], BF16)
            xbc = self.sb(ph, "xbc", [128, 5, 512], BF16)
            dtr = self.sb(ph, "dtr", [12, 512], F32)
            cqs = [self.sb(ph, "cq", [128, 3, 512], F32) for _ in range(2)]
            ckvs = [self.sb(ph, "ckv", [128, 2, 512], F32) for _ in range(2)]
            kropes = [self.sb(ph, "krope", [32, 512], BF16) for _ in range(2)]
            sq2 = self.sb(ph, "sq2", [128, 3, 512], BF16)
            fm = [self.sb(ph, "fm", [128, 512], F32) for _ in range(2)]
            gt = [self.sb(ph, "gt", [128, 4, 512], BF16) for _ in range(2)]
            f = [self.sb(ph, "ftmp", [128, 512], F32) for _ in range(4)]
            vsq = self.sb(ph, "vsq", [128, 3, 512], BF16)
            vn = self.sb(ph, "vn", [128, 3, 512], BF16)
            vT = self.sb(ph, "vT", [128, 4, 384], BF16)
            bout = self.sb(ph, "bout", [128, 3, 512], BF16)
            cqn = self.sb(ph, "cqn", [128, 3, 512], BF16)
            ckvn = self.sb(ph, "ckvn", [128, 2, 512], BF16)
            qraw = self.sb(ph, "qraw", [96, 512], BF16)
            qsq = self.sb(ph, "qsq", [96, 512], BF16)
            A1 = self.sb(ph, "A1", [96, 512], F32)
            A2 = self.sb(ph, "A2", [96, 512], F32)
            qout = [self.sb(ph, "qout", [96, 512], BF16) for _ in range(2)]
            cs = self.sb(ph, "cs", [96, 2, 512], F32)
            ve = self.sb(ph, "ve", [128, 6, 128], BF16)
            kb.memset(ve[:], 1.0, ve.d)
            pm = [self.ps(ph, "pm", [128, 512], F32) for _ in range(3)]
            pst = [self.ps(ph, "pst", [128, 512], F32) for _ in range(2)]
            px = [self.ps(ph, "px", [128, 512], F32) for _ in range(2)]
            ptb = self.ps(ph, "ptb", [128, 1024], BF16)
            ones, ident = self.ones_bf, self.ident_bf
            cd = self.cstb.d

            slabs = self.win_slabs()

            st_ = dict(nload=0, pmi=0)

            def main(ti):
                t0, TT, isctx = cfg.tiles[ti]
                zg, cq, ckv, krope = zgs[ti % 2], cqs[ti % 2], ckvs[ti % 2], kropes[ti % 2]
                j = 1 if isctx else 0
                x = xt[ti % 2]
                kb.dma("sp", x[:, :, :TT], xv[:, :, t0:t0 + TT], reads=[xsrc.d[ti] if l else xsrc.d[0]], writes=x.d)
                self.norm_mod(x, TT, j, self.S1, 0, sq, h, pst[0], fm[0], fm[1])
                for si_, (c0, ncol, mts) in enumerate(slabs):
                    w = wsl[st_['nload'] % 3]
                    st_['nload'] += 1
                    wv_ = w[:].rearrange("r k c -> r (k c)")[:, 0:8 * ncol].rearrange("r (k c) -> r k c", c=ncol)
                    kb.dma("sp", wv_, win[:, 8 * c0:8 * (c0 + ncol)].rearrange("r (k c) -> r k c", c=ncol), reads=[D_["WIN"].d[l * 16 + si_]], writes=w.d)
                    gtile = None
                    for mi, (off, msz, kind, idx) in enumerate(mts):
                        p = pm[st_['pmi'] % 2]
                        st_['pmi'] += 1
                        for k in range(8):
                            kb.mm(p[:msz, :TT], wv_[:, k, off:off + msz], h[:, k, :TT], k == 0, k == 7, w.d + h.d, p.d)
                        src = p[:msz, :TT]
                        if kind == "s5":
                            kb.cp(us[:, idx, :TT], src, p.d, us.d, eng="act")
                        elif kind == "sgu":
                            kb.act(zg[:, idx, :TT], src, AF.Gelu_apprx_tanh, p.d, [zg.d[idx]])
                        elif kind == "z":
                            kb.act(zs[:, idx, :TT], src, AF.Silu, p.d, zs.d)
                        elif kind == "xbc":
                            kb.cp(xbc[:, idx, :TT], src, p.d, xbc.d, eng="act")
                        elif kind == "dt":
                            kb.cp(dtr[:, :TT], src, p.d, dtr.d, eng="act")
                        elif kind == "cq":
                            kb.cp(cq[:, idx, :TT], src, p.d, cq.d, eng="act")
                        elif kind == "ckv":
                            kb.cp(ckv[:, idx, :TT], src, p.d, ckv.d, eng="act")
                        elif kind == "rope":
                            kb.cp(krope[:, :TT], src, p.d, krope.d, eng="act")
                        elif kind == "gate":
                            gtile = gt[idx % 2]
                            kb.act(gtile[:, mi, :TT], src, AF.Sigmoid, p.d, gtile.d)
                    if gtile is not None:
                        g = mts[0][3]
                        kb.dma("pool", D_["GATES"][g * 512:(g + 1) * 512, t0:t0 + TT].rearrange("(m r) t -> r m t", r=128),
                               gtile[:, :, :TT], reads=gtile.d, writes=[D_["GATES"].d[ti]])
                    yield
                kb.dma("pool", D_["US5"][:, t0:t0 + TT].rearrange("(m r) t -> r m t", r=128), us[:, :, :TT],
                       reads=us.d, writes=[D_["US5"].d[ti]])
                kb.dma("pool", D_["ZS"][:, :, t0:t0 + TT].rearrange("h r t -> r h t"), zs[:, :, :TT],
                       reads=zs.d, writes=[D_["ZS"].d[ti]])
                kb.dma("pool", D_["XBC"][:, t0:t0 + TT].rearrange("(m r) t -> r m t", r=128), xbc[:, :, :TT],
                       reads=xbc.d, writes=[D_["XBC"].d[ti]])
                kb.dma("pool", D_["DTR"][:, t0:t0 + TT], dtr[:, :TT], reads=dtr.d, writes=[D_["DTR"].d[ti]])


            def tail(ti):
                t0, TT, isctx = cfg.tiles[ti]
                zg, cq, ckv, krope = zgs[ti % 2], cqs[ti % 2], ckvs[ti % 2], kropes[ti % 2]
                nb = TT // 128
                kb.tt(vsq[:, :, :TT], zg[:, 3:6, :TT], zg[:, 3:6, :TT], ALU.mult, zg.d[3:6], vsq.d, eng="pool")
                for i in range(3):
                    kb.mm(pst[1][:, :TT], ones, zg[:, 3 + i, :TT], i == 0, i == 2, [zg.d[3 + i]] + cd, pst[1].d)
                for i in range(3):
                    kb.mm(px[0][:, :TT], ones, vsq[:, i, :TT], i == 0, i == 2, vsq.d + cd, px[0].d)
                mu, m2, var, rs = f[0], f[1], f[2], f[3]
                kb.ts(mu[:, :TT], pst[1][:, :TT], 1.0 / BRW, ALU.mult, pst[1].d, mu.d)
                kb.tt(m2[:, :TT], mu[:, :TT], mu[:, :TT], ALU.mult, mu.d, m2.d)
                kb.stt(var[:, :TT], px[0][:, :TT], 1.0 / BRW, m2[:, :TT], ALU.mult, ALU.subtract, px[0].d + m2.d, var.d)
                kb.act(m2[:, :TT], var[:, :TT], AF.Sqrt, var.d, m2.d, bias=1e-5)
                kb.recip(rs[:, :TT], m2[:, :TT], m2.d, rs.d)
                for i in range(3):
                    kb.tt(var[:, :TT], zg[:, 3 + i, :TT], mu[:, :TT], ALU.subtract, [zg.d[3 + i]] + mu.d, var.d)
                    kb.tt(var[:, :TT], var[:, :TT], rs[:, :TT], ALU.mult, var.d + rs.d, var.d)
                    kb.act(vn[:, i, :TT], var[:, :TT], AF.Identity, var.d + R["sguln"].d, vn.d,
                           scale=R["sguln"][:, i:i + 1], bias=R["sguln"][:, 3 + i:4 + i])
                for jb in range(nb):
                    for i in range(3):
                        kb.tr(ptb[:, i * 128:(i + 1) * 128], vn[:, i, jb * 128:(jb + 1) * 128], ident, vn.d + cd, ptb.d)
                    kb.cp(vT[:, jb, :], ptb[:, 0:384], ptb.d, vT.d, eng="act")
                    yield
                for c in range(3):
                    for half in range(2):
                        hh = 2 * c + half
                        pp = px[half]
                        for jb in range(nb):
                            kb.mm(pp[:, jb * 128:(jb + 1) * 128], vT[:, jb, c * 128:(c + 1) * 128], R["WsT"][:, hh, :],
                                  True, True, vT.d + R["WsT"].d, pp.d)
                        rsl = slice(half * 64, half * 64 + 64)
                        bsb = R["BS"][rsl, c, :].unsqueeze(1).broadcast_to([64, nb, 128])
                        tmp = f[2]
                        kb.tt(tmp[rsl, :TT].rearrange("p (j t) -> p j t", t=128),
                              pp[rsl, :TT].rearrange("p (j t) -> p j t", t=128), bsb, ALU.add,
                              pp.d + R["BS"].d, tmp.d)
                        kb.tt(bout[rsl, c, :TT], tmp[rsl, :TT], zg[rsl, c, :TT], ALU.mult, tmp.d + [zg.d[c]], bout.d, eng="pool")
                        yield
                kb.dma("pool", D_["YB"][:, t0:t0 + TT].rearrange("(m r) t -> r m t", r=128), bout[:, :, :TT],
                       reads=bout.d, writes=[D_["YB"].d[ti]])

                mc = R["mcol"]
                for (src_t, nck, dst, pstx, gcol) in ((cq, 3, cqn, pst[1], 0), (ckv, 2, ckvn, px[0], 3)):
                    kb.act(sq2[:, :nck, :TT], src_t[:, :, :TT], AF.Square, src_t.d, sq2.d)
                    for i in range(nck):
                        kb.mm(pstx[:, :TT], ones, sq2[:, i, :TT], i == 0, i == nck - 1, sq2.d + cd, pstx.d)
                    kb.act(f[0][:, :TT], pstx[:, :TT], AF.Sqrt, pstx.d, f[0].d, scale=1.0 / (128 * nck), bias=1e-6)
                    kb.recip(f[1][:, :TT], f[0][:, :TT], f[0].d, f[1].d)
                    for i in range(nck):
                        kb.stt(dst[:, i, :TT], src_t[:, i, :TT], mc[:, gcol + i:gcol + i + 1], f[1][:, :TT],
                               ALU.mult, ALU.mult, src_t.d + mc.d + f[1].d, dst.d)
                if not isctx:
                    kb.dma("sp", cs[:, :, :TT], D_["rope_cs"][:, :, t0 - CTX:t0 - CTX + TT].rearrange("a r t -> r a t"),
                           reads=(), writes=cs.d)
                nq = 0
                for which in range(2):
                    for hh in range(6):
                        pq, pss, psw = pm[2], px[0], px[1]
                        if which == 0:
                            for i in range(3):
                                kb.mm(pq[:96, :TT], R["Wuq"][:, i, hh * 96:(hh + 1) * 96], cqn[:, i, :TT], i == 0, i == 2,
                                      R["Wuq"].d + cqn.d, pq.d)
                            gc = 5
                        else:
                            for i in range(2):
                                kb.mm(pq[:96, :TT], R["Wk"][:, i, hh, :], ckvn[:, i, :TT], i == 0, False,
                                      R["Wk"].d + ckvn.d, pq.d)
                            kb.mm(pq[:96, :TT], self.cstb[0:32, 3, 0:96], krope[:, :TT], False, True, cd + krope.d, pq.d)
                            gc = 7
                        kb.cp(qraw[:, :TT], pq[:96, :TT], pq.d, qraw.d, eng="act")
                        kb.act(qsq[:, :TT], pq[:96, :TT], AF.Square, pq.d, qsq.d)
                        kb.mm(pss[:96, :TT], self.cstb[0:96, 0, 0:96], qsq[:, :TT], True, True, cd + qsq.d, pss.d)
                        kb.act(f[0][:96, :TT], pss[:96, :TT], AF.Sqrt, pss.d, f[0].d, scale=1.0 / 96, bias=1e-6)
                        kb.recip(f[1][:96, :TT], f[0][:96, :TT], f[0].d, f[1].d)
                        kb.act(A1[:, :TT], pq[:96, :TT], AF.Identity, pq.d + mc.d, A1.d, scale=mc[0:96, gc:gc + 1])
                        if not isctx:
                            kb.mm(psw[:96, :TT], self.cstb[0:96, 2, 0:96], qraw[:, :TT], True, True, cd + qraw.d, psw.d)
                            kb.act(A2[:, :TT], psw[:96, :TT], AF.Identity, psw.d + mc.d, A2.d, scale=mc[0:96, gc + 1:gc + 2])
                            kb.tt(A1[:, :TT], A1[:, :TT], cs[:, 0, :TT], ALU.mult, A1.d + cs.d, A1.d)
                            kb.tt(A2[:, :TT], A2[:, :TT], cs[:, 1, :TT], ALU.mult, A2.d + cs.d, A2.d, eng="pool")
                            kb.tt(A1[:, :TT], A1[:, :TT], A2[:, :TT], ALU.add, A1.d + A2.d, A1.d)
                        qo = qout[nq % 2]
                        nq += 1
                        kb.tt(qo[:, :TT], A1[:, :TT], f[1][:96, :TT], ALU.mult, A1.d + f[1].d, qo.d)
                        dst = D_["QT"] if which == 0 else D_["KT"]
                        kb.dma("pool", dst[hh, :, t0:t0 + TT], qo[:, :TT], reads=qo.d, writes=[dst.d[ti]])
                        yield
                for jb in range(nb):
                    pv = pm[2]
                    for i in range(2):
                        kb.mm(pv[:, 0:384], ckvn[:, i, jb * 128:(jb + 1) * 128], R["Wv"][:, i, :, :], i == 0, i == 1,
                              ckvn.d + R["Wv"].d, pv.d)
                    kb.cp(ve[:, :, 0:64], pv[:, 0:384].rearrange("p (h c) -> p h c", c=64), pv.d, ve.d, eng="act")
                    kb.dma("pool", D_["VE"][(t0 // 128) + jb].rearrange("p (h c) -> p h c", c=128), ve[:], reads=ve.d,
                           writes=[D_["VE"].d[ti]])

            def drive(gm, gt_):
                dm = gm is None
                dt_ = gt_ is None
                while not (dm and dt_):
                    if not dm:
                        try:
                            next(gm)
                        except StopIteration:
                            dm = True
                    for _ in range(2):
                        if dt_:
                            break
                        try:
                            next(gt_)
                        except StopIteration:
                            dt_ = True

            nt_ = len(cfg.tiles)
            drive(main(0), None)
            for ti in range(1, nt_):
                drive(main(ti), tail(ti - 1))
            drive(None, tail(nt_ - 1))
            kb.barrier()

    def _wrap_pi(self, x, m, n=4):
        kb = self.kb
        for _ in range(n):
            kb.ts(m, x, PI, ALU.is_gt, [self._wd], [self._wm], s2=2 * PI, op1=ALU.mult)
            kb.tt(x, x, m, ALU.subtract, [self._wd, self._wm], [self._wd])

    def s5_setup(self, l, R, ph):
        kb, D_ = self.kb, self.dr
        W = 24 * 128
        R["BDR"] = self.sb(ph, "BDR", [128, 24, 128], BF16)
        R["BDI"] = self.sb(ph, "BDI", [128, 24, 128], BF16)
        R["CR"] = self.sb(ph, "CR", [128, 24, 128], BF16)
        R["CIn"] = self.sb(ph, "CIn", [128, 24, 128], BF16)
        R["rcol"] = self.sb(ph, "rcol", [128, 24], F32)
        R["dcol"] = self.sb(ph, "dcol", [128, 3], F32)
        kb.dma("sp", R["dcol"][:], D_["s5_dcol"][l], reads=(), writes=R["dcol"].d)
        R["Wglu"] = self.sb(ph, "Wglu", [128, 3, 384], BF16)
        kb.dma("pool", R["Wglu"][:], D_["s5_w_glu"][l].rearrange("(i r) c -> r i c", r=128), reads=(), writes=R["Wglu"].d)
        with ExitStack() as p2:
            T = [self.sb(p2, "s5t", [128, W], F32) for _ in range(10)]
            aR, aI, t1, bR, bI, t2, t3, t4, t5, t6 = T
            rep = D_["s5_rep"][l]
            for i, t in enumerate((aR, aI, t1, bR, bI)):
                kb.dma("sp", t[:], rep[i].rearrange("p s c -> p (s c)"), reads=(), writes=t.d)
            kb.act(t1[:], t1[:], AF.Exp, t1.d, t1.d)
            kb.tt(t2[:], aR[:], t1[:], ALU.mult, aR.d + t1.d, t2.d)
            kb.tt(t3[:], aI[:], t1[:], ALU.mult, aI.d + t1.d, t3.d)
            kb.act(t2[:], t2[:], AF.Exp, t2.d, t2.d)
            kb.ts(t4[:], t3[:], PI / 2, ALU.add, t3.d, t4.d)
            for x in (t3, t4):
                self._wd, self._wm = x.d[0], t5.d[0]
                self._wrap_pi(x[:], t5[:])
            kb.act(t3[:], t3[:], AF.Sin, t3.d, t3.d)
            kb.act(t4[:], t4[:], AF.Sin, t4.d, t4.d)
            kb.tt(t4[:], t2[:], t4[:], ALU.mult, t2.d + t4.d, t4.d)
            kb.tt(t3[:], t2[:], t3[:], ALU.mult, t2.d + t3.d, t3.d)
            kb.tt(t1[:], aR[:], aR[:], ALU.mult, aR.d, t1.d)
            kb.tt(t5[:], aI[:], aI[:], ALU.mult, aI.d, t5.d)
            kb.tt(t1[:], t1[:], t5[:], ALU.add, t1.d + t5.d, t1.d)
            kb.recip(t1[:], t1[:], t1.d, t1.d)
            kb.ts(t4[:], t4[:], -1.0, ALU.add, t4.d, t4.d)
            kb.tt(t5[:], t4[:], aR[:], ALU.mult, t4.d + aR.d, t5.d)
            kb.tt(t6[:], t3[:], aI[:], ALU.mult, t3.d + aI.d, t6.d)
            kb.tt(t5[:], t5[:], t6[:], ALU.add, t5.d + t6.d, t5.d)
            kb.tt(t5[:], t5[:], t1[:], ALU.mult, t5.d + t1.d, t5.d)
            kb.tt(t6[:], t3[:], aR[:], ALU.mult, t3.d + aR.d, t6.d)
            kb.tt(t2[:], t4[:], aI[:], ALU.mult, t4.d + aI.d, t2.d)
            kb.tt(t6[:], t6[:], t2[:], ALU.subtract, t6.d + t2.d, t6.d)
            kb.tt(t6[:], t6[:], t1[:], ALU.mult, t6.d + t1.d, t6.d)
            kb.tt(t2[:], t5[:], bR[:], ALU.mult, t5.d + bR.d, t2.d)
            kb.tt(t3[:], t6[:], bI[:], ALU.mult, t6.d + bI.d, t3.d)
            kb.tt(R["BDR"][:].rearrange("p s c -> p (s c)"), t2[:], t3[:], ALU.subtract, t2.d + t3.d, R["BDR"].d)
            kb.tt(t2[:], t5[:], bI[:], ALU.mult, t5.d + bI.d, t2.d)
            kb.tt(t3[:], t6[:], bR[:], ALU.mult, t6.d + bR.d, t3.d)
            kb.tt(R["BDI"][:].rearrange("p s c -> p (s c)"), t2[:], t3[:], ALU.add, t2.d + t3.d, R["BDI"].d)
            cpl = D_["s5_cpl"][l]
            kb.dma("sp", aR[:], cpl[0].rearrange("p s c -> p (s c)"), reads=(), writes=aR.d)
            kb.dma("sp", aI[:], cpl[1].rearrange("p s c -> p (s c)"), reads=(), writes=aI.d)
            kb.cp(R["CR"][:].rearrange("p s c -> p (s c)"), aR[:], aR.d, R["CR"].d, eng="act")
            kb.ts(R["CIn"][:].rearrange("p s c -> p (s c)"), aI[:], -1.0, ALU.mult, aI.d, R["CIn"].d)
            kb.barrier()
        with ExitStack() as p2:
            cl = self.sb(p2, "s5col", [128, 3, 24], F32)
            kb.dma("sp", cl[:], D_["s5_col"][l].rearrange("a p s -> p a s"), reads=(), writes=cl.d)
            dt = self.sb(p2, "c_dt", [128, 24], F32)
            th = self.sb(p2, "c_th", [128, 24], F32)
            thc = self.sb(p2, "c_thc", [128, 24], F32)
            m = self.sb(p2, "c_m", [128, 24], F32)
            PK = self.sb(p2, "PK", [128, 2, 10, 24], F32)
            kb.act(dt[:], cl[:, 2, :], AF.Exp, cl.d, dt.d)
            kb.tt(th[:], cl[:, 0, :], dt[:], ALU.mult, cl.d + dt.d, th.d)
            kb.act(R["rcol"][:], th[:], AF.Exp, th.d, R["rcol"].d)
            kb.tt(th[:], cl[:, 1, :], dt[:], ALU.mult, cl.d + dt.d, th.d)
            kb.ts(thc[:], th[:], PI / 2, ALU.add, th.d, thc.d)
            for x in (th, thc):
                self._wd, self._wm = x.d[0], m.d[0]
                self._wrap_pi(x[:], m[:])
            kb.act(PK[:, 1, 0, :], th[:], AF.Sin, th.d, PK.d)
            kb.act(PK[:, 0, 0, :], thc[:], AF.Sin, thc.d, PK.d)
            for k in range(9):
                pr, pi_ = PK[:, 0, k, :], PK[:, 1, k, :]
                kb.tt(dt[:], pr, pr, ALU.mult, PK.d, dt.d)
                kb.tt(m[:], pi_, pi_, ALU.mult, PK.d, m.d)
                kb.tt(PK[:, 0, k + 1, :], dt[:], m[:], ALU.subtract, dt.d + m.d, PK.d)
                kb.tt(dt[:], pr, pi_, ALU.mult, PK.d, dt.d)
                kb.ts(PK[:, 1, k + 1, :], dt[:], 2.0, ALU.mult, dt.d, PK.d)
            Tr = self.sb(p2, "Tr", [128, 12, 512], F32)
            Ti = self.sb(p2, "Ti", [128, 12, 512], F32)
            m1 = self.sb(p2, "m1", [128, 12, 256], F32)
            m2 = self.sb(p2, "m2", [128, 12, 256], F32)
            for hf in range(2):
                sl = slice(hf * 12, hf * 12 + 12)
                kb.cp(Tr[:, :, 0], PK[:, 0, 0, sl], PK.d, Tr.d)
                kb.cp(Ti[:, :, 0], PK[:, 1, 0, sl], PK.d, Ti.d)
                for k in range(9):
                    w = 1 << k
                    prb = PK[:, 0, k, sl].unsqueeze(2).broadcast_to([128, 12, w])
                    pib = PK[:, 1, k, sl].unsqueeze(2).broadcast_to([128, 12, w])
                    kb.tt(m1[:, :, :w], Tr[:, :, 0:w], prb, ALU.mult, Tr.d + PK.d, m1.d)
                    kb.tt(m2[:, :, :w], Ti[:, :, 0:w], pib, ALU.mult, Ti.d + PK.d, m2.d)
                    kb.tt(Tr[:, :, w:2 * w], m1[:, :, :w], m2[:, :, :w], ALU.subtract, m1.d + m2.d, Tr.d)
                    kb.tt(m1[:, :, :w], Tr[:, :, 0:w], pib, ALU.mult, Tr.d + PK.d, m1.d)
                    kb.tt(m2[:, :, :w], Ti[:, :, 0:w], prb, ALU.mult, Ti.d + PK.d, m2.d)
                    kb.tt(Ti[:, :, w:2 * w], m1[:, :, :w], m2[:, :, :w], ALU.add, m1.d + m2.d, Ti.d)
                tb = D_["TBL"]
                kb.dma("sp", tb[sl, :, 0, :].rearrange("s p t -> p s t"), Tr[:], reads=Tr.d, writes=tb.d[sl])
                kb.dma("sp", tb[sl, :, 1, :].rearrange("s p t -> p s t"), Ti[:], reads=Ti.d, writes=tb.d[sl])
            kb.barrier()

    def gen_s5(self, l, ph):
        kb, cfg, D_ = self.kb, self.cfg, self.dr
        ntile = len(cfg.tiles)
        if True:
            R = {}
            self.s5_setup(l, R, ph)
            u = [self.sb(ph, "s5u", [128, 3, 512], BF16) for _ in range(2)]
            tb = [self.sb(ph, "s5tb", [128, 2, 512], F32) for _ in range(3)]
            WS = [dict(P=[self.sb(ph, "s5P", [128, 512], F32) for _ in range(4)],
                       mr=self.sb(ph, "s5mr", [128, 512], F32), mi=self.sb(ph, "s5mi", [128, 512], F32),
                       gr=self.sb(ph, "s5gr", [128, 512], F32), gi=self.sb(ph, "s5gi", [128, 512], F32))
                  for _ in range(2)]
            HS = [(self.sb(ph, "s5hr", [128, 512], F32), self.sb(ph, "s5hi", [128, 512], F32)) for _ in range(4)]
            pend = []
            LAG = 3
            hb = [self.sb(ph, "s5hb", [128, 2, 512], BF16) for _ in range(2)]
            ysb = [self.sb(ph, "s5ys", [128, 512], F32) for _ in range(2)]
            hst = self.sb(ph, "s5hst", [128, 24, 2], F32, n=24)
            pxr = [self.ps(ph, "pxr", [128, 512], F32) for _ in range(1)]
            pxi = [self.ps(ph, "pxi", [128, 512], F32) for _ in range(1)]
            py = [self.ps(ph, "py", [128, 512], F32) for _ in range(1)]

            def rv(ap_t, TT, rev):
                if not rev:
                    return ap_t[:, 0:TT]
                return ap_t[:, TT - 1::-1] if TT < 512 else ap_t[:, ::-1]

            cnt = 0
            for d in range(2):
                kb.memset(hst[:], 0.0, hst.d)
                order = list(range(ntile)) if d == 0 else [0] + list(range(ntile - 1, 0, -1))
                rev = d == 1
                for oi, ti in enumerate(order):
                    t0, TT, isctx = cfg.tiles[ti]
                    ut = u[oi % 2]
                    kb.dma("sp", ut[:, :, :TT], D_["US5"][:, t0:t0 + TT].rearrange("(m r) t -> r m t", r=128),
                           reads=[D_["US5"].d[ti]], writes=ut.d)
                    for ch in range(3):
                        pyt = py[0]
                        for qq in range(4):
                            q = 4 * ch + qq
                            st = d * 12 + q
                            tbl = tb[cnt % 3]
                            xr, xi = pxr[0], pxi[0]
                            ws = WS[cnt % 2]
                            P_, mr, mi, gr, gi = ws['P'], ws['mr'], ws['mi'], ws['gr'], ws['gi']
                            hr, hi = HS[cnt % 4]
                            hbt = hb[cnt % 2]
                            cnt += 1
                            kb.dma("sp", tbl[:], D_["TBL"][st], reads=[D_["TBL"].d[st]], writes=tbl.d)
                            kb.mm(xr[:, :TT], R["BDR"][:, st, :], ut[:, ch, :TT], True, True, R["BDR"].d + ut.d, xr.d)
                            kb.mm(xi[:, :TT], R["BDI"][:, st, :], ut[:, ch, :TT], True, True, R["BDI"].d + ut.d, xi.d)
                            c_, s_ = tbl[:, 0, :TT], tbl[:, 1, :TT]
                            xrv, xiv = rv(xr, TT, rev), rv(xi, TT, rev)
                            kb.tt(P_[0][:, :TT], xrv, c_, ALU.mult, xr.d + tbl.d, P_[0].d)
                            kb.tt(P_[1][:, :TT], xiv, s_, ALU.mult, xi.d + tbl.d, P_[1].d)
                            kb.tt(mr[:, :TT], P_[0][:, :TT], P_[1][:, :TT], ALU.add, P_[0].d + P_[1].d, mr.d)
                            kb.tt(P_[2][:, :TT], xiv, c_, ALU.mult, xi.d + tbl.d, P_[2].d)
                            kb.tt(P_[3][:, :TT], xrv, s_, ALU.mult, xr.d + tbl.d, P_[3].d)
                            kb.tt(mi[:, :TT], P_[2][:, :TT], P_[3][:, :TT], ALU.subtract, P_[2].d + P_[3].d, mi.d)
                            rb = R["rcol"][:, st:st + 1].broadcast_to([128, TT])
                            hd = [hst.d[st]]
                            kb.op("dve", lambda e, a=gr[:, :TT], b=rb, c=mr[:, :TT], i0=hst[:, st, 0:1]:
                                  e.tensor_tensor_scan(a, b, c, i0, ALU.mult, ALU.add), R["rcol"].d + mr.d + hd, gr.d)
                            kb.op("dve", lambda e, a=gi[:, :TT], b=rb, c=mi[:, :TT], i0=hst[:, st, 1:2]:
                                  e.tensor_tensor_scan(a, b, c, i0, ALU.mult, ALU.add), R["rcol"].d + mi.d + hd, gi.d)
                            kb.tt(P_[0][:, :TT], gr[:, :TT], c_, ALU.mult, gr.d + tbl.d, P_[0].d, eng="pool")
                            kb.tt(P_[1][:, :TT], gi[:, :TT], s_, ALU.mult, gi.d + tbl.d, P_[1].d, eng="pool")
                            kb.tt(hr[:, :TT], P_[0][:, :TT], P_[1][:, :TT], ALU.subtract, P_[0].d + P_[1].d, hr.d, eng="pool")
                            kb.tt(P_[2][:, :TT], gr[:, :TT], s_, ALU.mult, gr.d + tbl.d, P_[2].d, eng="pool")
                            kb.tt(P_[3][:, :TT], gi[:, :TT], c_, ALU.mult, gi.d + tbl.d, P_[3].d, eng="pool")
                            kb.tt(hi[:, :TT], P_[2][:, :TT], P_[3][:, :TT], ALU.add, P_[2].d + P_[3].d, hi.d, eng="pool")
                            kb.cp(hst[:, st, 0:1], hr[:, TT - 1:TT], hr.d, hd, eng="pool")
                            kb.cp(hst[:, st, 1:2], hi[:, TT - 1:TT], hi.d, hd, eng="pool")
                            def back(hbt=hbt, hr=hr, hi=hi, TT=TT, rev=rev, pyt=pyt, st=st, qq=qq, ch=ch, d=d, t0=t0, ti=ti):
                                kb.cp(rv(hbt[:, 0, :], TT, rev), hr[:, :TT], hr.d, hbt.d, eng="act")
                                kb.cp(rv(hbt[:, 1, :], TT, rev), hi[:, :TT], hi.d, hbt.d, eng="act")
                                kb.mm(pyt[:, :TT], R["CR"][:, st, :], hbt[:, 0, :TT], qq == 0, False, R["CR"].d + hbt.d, pyt.d)
                                kb.mm(pyt[:, :TT], R["CIn"][:, st, :], hbt[:, 1, :TT], False, qq == 3, R["CIn"].d + hbt.d, pyt.d)
                                if qq == 3:
                                    yt = ysb[ch % 2]
                                    kb.cp(yt[:, :TT], pyt[:, :TT], pyt.d, yt.d, eng="act")
                                    kb.dma("sp", D_["YS5"][d, ch * 128:(ch + 1) * 128, t0:t0 + TT], yt[:, :TT], reads=yt.d,
                                           writes=[D_["YS5"].d[d * ntile + ti]])
                            pend.append(back)
                            if len(pend) > LAG:
                                pend.pop(0)()
                            yield
            while pend:
                pend.pop(0)()
            y0 = self.sb(ph, "s5y0", [128, 3, 512], F32)
            y1 = self.sb(ph, "s5y1", [128, 3, 512], F32)
            ya = self.sb(ph, "s5ya", [128, 3, 512], BF16)
            yo = self.sb(ph, "s5yo", [128, 3, 512], BF16)
            sg = self.sb(ph, "s5sg", [128, 512], F32)
            for ti, (t0, TT, isctx) in enumerate(cfg.tiles):
                ut = u[ti % 2]
                kb.dma("sp", ut[:, :, :TT], D_["US5"][:, t0:t0 + TT].rearrange("(m r) t -> r m t", r=128),
                       reads=[D_["US5"].d[ti]], writes=ut.d)
                kb.dma("sp", y0[:, :, :TT], D_["YS5"][0, :, t0:t0 + TT].rearrange("(m r) t -> r m t", r=128),
                       reads=[D_["YS5"].d[ti]], writes=y0.d)
                kb.dma("sp", y1[:, :, :TT], D_["YS5"][1, :, t0:t0 + TT].rearrange("(m r) t -> r m t", r=128),
                       reads=[D_["YS5"].d[ntile + ti]], writes=y1.d)
                for i in range(3):
                    kb.stt(y0[:, i, :TT], ut[:, i, :TT], R["dcol"][:, i:i + 1], y0[:, i, :TT], ALU.mult, ALU.add,
                           ut.d + R["dcol"].d + y0.d, y0.d)
                kb.tt(y0[:, :, :TT], y0[:, :, :TT], y1[:, :, :TT], ALU.add, y0.d + y1.d, y0.d)
                kb.act(ya[:, :, :TT], y0[:, :, :TT], AF.Gelu_apprx_tanh, y0.d, ya.d)
                for m_ in range(3):
                    pg = py[0]
                    for i in range(3):
                        kb.mm(pg[:, :TT], R["Wglu"][:, i, m_ * 128:(m_ + 1) * 128], ya[:, i, :TT], i == 0, i == 2,
                              R["Wglu"].d + ya.d, pg.d)
                    kb.act(sg[:, :TT], pg[:, :TT], AF.Sigmoid, pg.d, sg.d)
                    kb.tt(yo[:, m_, :TT], ya[:, m_, :TT], sg[:, :TT], ALU.mult, ya.d + sg.d, yo.d)
                kb.dma("sp", D_["YA"][:, t0:t0 + TT].rearrange("(m r) t -> r m t", r=128), yo[:, :, :TT], reads=yo.d,
                       writes=[D_["YA"].d[ti]])
                yield

    def pass_s5(self, l):
        with ExitStack() as ph:
            for _ in self.gen_s5(l, ph):
                pass
            self.kb.barrier()

    def pass_ssd(self, l):
        kb, cfg, D_ = self.kb, self.cfg, self.dr
        ntile = len(cfg.tiles)
        NB = cfg.NB
        cst = self.cst
        Mgt, Ule, Mlt, Uge = cst[:, 4, :], cst[:, 5, :], cst[:, 6, :], cst[:, 7, :]
        onesf = cst[:, 0, :]
        cd, cdb = cst.d, self.cstb.d
        with ExitStack() as ph:
            cw = self.sb(ph, "ssd_cw", [128, 5, 4], F32)
            kb.dma("sp", cw[:], D_["ssd_cw"][l], reads=(), writes=cw.d)
            srep = self.sb(ph, "ssd_rep", [128, 30], F32)
            kb.dma("sp", srep[:], D_["ssd_rep"][l], reads=(), writes=srep.d)
            ng = self.sb(ph, "ssd_ng", [64, 6], F32)
            kb.dma("sp", ng[:], D_["ssd_ng"][l], reads=(), writes=ng.d)
            Arep = self.sb(ph, "Arep", [128, 12], F32)
            kb.act(Arep[:], srep[:, 0:12], AF.Exp, srep.d, Arep.d)
            kb.ts(Arep[:], Arep[:], -1.0, ALU.mult, Arep.d, Arep.d)
            dtb, drep = srep[:, 12:24], srep[:, 24:30]

            xwin = self.sb(ph, "xwin", [128, 5, 514], BF16)
            acc = self.sb(ph, "cacc", [128, 512], F32)
            xc = [self.sb(ph, "xc", [128, 5, 512], BF16) for _ in range(2)]
            dtr = [self.sb(ph, "sdtr", [12, 512], F32) for _ in range(2)]
            zs = [self.sb(ph, "szs", [64, 6, 512], BF16) for _ in range(2)]
            def mkset():
                return dict(
                    dtT=self.sb(ph, "dtT", [128, 12], F32), dA=self.sb(ph, "dA", [128, 12], F32),
                    wv=self.sb(ph, "wv", [128, 6], F32), dtw=self.sb(ph, "dtw", [128, 6], F32),
                    decc=self.sb(ph, "decc", [128, 6], F32), xT=self.sb(ph, "xT", [128, 6, 64], BF16),
                    BT=self.sb(ph, "BT", [128, 128], BF16), xw=self.sb(ph, "xw", [128, 6, 64], BF16),
                    xdt=[self.sb(ph, "xdt", [128, 6, 64], BF16) for _ in range(2)],
                    xd=self.sb(ph, "xd", [128, 6, 64], BF16))
            csets = [mkset() for _ in range(2)]
            dtT = dA = wv = dtw = decc = xT = BT = xw = xdt = xd = None
            S = [self.sb(ph, "Sst", [128, 6, 64], F32) for _ in range(2)]
            Sbf = [self.sb(ph, "Sbf", [128, 6, 64], BF16) for _ in range(2)]
            Sin = [self.sb(ph, "Sinf", [128, 6, 64], BF16) for _ in range(2)]
            stmp = self.sb(ph, "stmp", [128, 6, 64], F32)
            GTs = [self.sb(ph, "GT", [128, 2, 128], F32) for _ in range(2)]
            BTms = [self.sb(ph, "BTm", [128, 2, 128], BF16) for _ in range(2)]
            for t_ in BTms:
                kb.memset(t_[:], 0.0, t_.d)
            GT = BTm = None
            Ld = [self.sb(ph, "Ld", [128, 128], F32) for _ in range(4)]
            dAb = [self.sb(ph, "dAb", [128, 128], F32) for _ in range(4)]
            E = [self.sb(ph, "Eseg", [128, 128], F32) for _ in range(4)]
            ea = [self.sb(ph, "ea", [128, 128], F32) for _ in range(4)]
            MT = [self.sb(ph, "MT", [128, 128], BF16) for _ in range(6)]
            CE = [self.sb(ph, "CE", [128, 128], BF16) for _ in range(6)]
            yg = self.sb(ph, "yg", [64, 6, 128], F32)
            ysq = self.sb(ph, "ysq", [64, 6, 128], BF16)
            rs1 = self.sb(ph, "rs1", [64, 128], F32)
            rs2 = self.sb(ph, "rs2", [64, 128], F32)
            yct = self.sb(ph, "yct", [64, 6, 512], BF16)
            pdt = self.ps(ph, "pdt", [128, 512], F32)
            ptb = self.ps(ph, "sptb", [128, 1024], BF16)
            pG = self.ps(ph, "pG", [128, 512], F32)
            pse = [self.ps(ph, "pse", [128, 512], F32) for _ in range(2)]
            py = [self.ps(ph, "spy", [128, 512], F32) for _ in range(2)]
            pss = self.ps(ph, "spst", [128, 512], F32)
            ident = self.ident_bf

            def dt_chunk(dtile, jb):
                kb.tr(pdt[:, 0:12], dtile[:, jb * 128:(jb + 1) * 128], cst[0:12, 1, 0:12], dtile.d + cd, pdt.d)
                kb.tt(dtT[:], pdt[:, 0:12], dtb, ALU.add, pdt.d + srep.d, dtT.d)
                kb.act(dtT[:], dtT[:], AF.Exp, dtT.d, dtT.d)
                kb.act(dtT[:], dtT[:], AF.Ln, dtT.d, dtT.d, bias=1.0)
                kb.tt(dA[:], dtT[:], Arep[:], ALU.mult, dtT.d + Arep.d, dA.d)

            def tok_major(xct, jb):
                sl = slice(jb * 128, (jb + 1) * 128)
                for k in range(4):
                    kb.tr(ptb[:, k * 128:(k + 1) * 128], xct[:, k, sl], ident, xct.d + cdb, ptb.d)
                kb.cp(xT[:].rearrange("p h c -> p (h c)"), ptb[:, 0:384], ptb.d, xT.d, eng="act")
                kb.cp(BT[:], ptb[:, 384:512], ptb.d, BT.d, eng="act")

            def state_update(d, mask_w, dec_ap, dec_deps):
                c0 = d * 6
                kb.mm(pss[:, 0:6], mask_w, dA[:, c0:c0 + 6], True, True, cd + dA.d, pss.d)
                kb.act(wv[:], pss[:, 0:6], AF.Exp, pss.d, wv.d)
                kb.tt(dtw[:], dtT[:, c0:c0 + 6], wv[:], ALU.mult, dtT.d + wv.d, dtw.d)
                kb.tt(xw[:], xT[:], dtw[:].unsqueeze(2).broadcast_to([128, 6, 64]), ALU.mult, xT.d + dtw.d, xw.d)
                kb.mm(pss[:, 128:512], BT[:], xw[:].rearrange("p h c -> p (h c)"), True, True, BT.d + xw.d, pss.d)
                kb.tt(stmp[:], S[d][:], dec_ap.unsqueeze(2).broadcast_to([128, 6, 64]), ALU.mult, S[d].d + dec_deps, stmp.d)
                kb.tt(S[d][:].rearrange("p h c -> p (h c)"), stmp[:].rearrange("p h c -> p (h c)"), pss[:, 128:512],
                      ALU.add, stmp.d + pss.d, S[d].d)
                kb.cp(Sbf[d][0:64, 0:3, :], S[d][0:64, 0:3, :], S[d].d, Sbf[d].d, eng="act")
                kb.cp(Sbf[d][64:128, 3:6, :], S[d][64:128, 3:6, :], S[d].d, Sbf[d].d, eng="act")

            kb.memset(S[0][:], 0.0, S[0].d)
            kb.memset(Sbf[0][:], 0.0, Sbf[0].d)
            for ti, (t0, TT, isctx) in enumerate(cfg.tiles):
                lz = isctx or ti == 1
                rz = isctx or ti == ntile - 1
                a = t0 - (0 if lz else 1)
                b = t0 + TT + (0 if rz else 1)
                if lz:
                    kb.memset(xwin[:, :, 0:1], 0.0, xwin.d)
                if rz:
                    kb.memset(xwin[:, :, TT + 1:TT + 2], 0.0, xwin.d)
                c0 = 1 if lz else 0
                rd = [D_["XBC"].d[ti]] + ([] if lz else [D_["XBC"].d[ti - 1]]) + ([] if rz else [D_["XBC"].d[ti + 1]])
                kb.dma("sp", xwin[:, :, c0:c0 + (b - a)], D_["XBC"][:, a:b].rearrange("(m r) t -> r m t", r=128),
                       reads=rd, writes=xwin.d)
                xct = xc[ti % 2]
                for k in range(5):
                    kb.act(acc[:, :TT], xwin[:, k, 1:TT + 1], AF.Identity, xwin.d + cw.d, acc.d,
                           scale=cw[:, k, 1:2], bias=cw[:, k, 3:4])
                    kb.stt(acc[:, :TT], xwin[:, k, 0:TT], cw[:, k, 0:1], acc[:, :TT], ALU.mult, ALU.add,
                           xwin.d + cw.d + acc.d, acc.d)
                    kb.stt(acc[:, :TT], xwin[:, k, 2:TT + 2], cw[:, k, 2:3], acc[:, :TT], ALU.mult, ALU.add,
                           xwin.d + cw.d + acc.d, acc.d)
                    kb.act(xct[:, k, :TT], acc[:, :TT], AF.Silu, acc.d, xct.d)
                kb.dma("pool", D_["XC"][:, t0:t0 + TT].rearrange("(m r) t -> r m t", r=128), xct[:, :, :TT], reads=xct.d,
                       writes=[D_["XC"].d[ti]])
                dtile = dtr[ti % 2]
                kb.dma("sp", dtile[:, :TT], D_["DTR"][:, t0:t0 + TT], reads=[D_["DTR"].d[ti]], writes=dtile.d)
                for jb in range(TT // 128):
                    c = t0 // 128 + jb
                    cs_ = csets[c % 2]
                    dtT, dA, wv, dtw, decc, xT, BT, xw, xdt, xd = (cs_[k_] for k_ in ('dtT', 'dA', 'wv', 'dtw', 'decc', 'xT', 'BT', 'xw', 'xdt', 'xd'))
                    GT, BTm = GTs[c % 2], BTms[c % 2]
                    kb.dma("pool", D_["SINF"][c], Sbf[0][:].rearrange("p h c -> p (h c)"), reads=Sbf[0].d,
                           writes=[D_["SINF"].d[c]])
                    dt_chunk(dtile, jb)
                    tok_major(xct, jb)
                    kb.mm(pse[0][:, 0:6], onesf, dA[:, 0:6], True, True, cd + dA.d, pse[0].d)
                    kb.act(decc[:], pse[0][:, 0:6], AF.Exp, pse[0].d, decc.d)
                    state_update(0, Mgt, decc[:], decc.d)
            kb.barrier()

            if cfg.stop_after == 'ssd1':
                return
            kb.memset(S[1][:], 0.0, S[1].d)
            kb.memset(Sbf[1][:], 0.0, Sbf[1].d)
            order = [0] + list(range(ntile - 1, 0, -1))
            nmt = 0
            for oi, ti in enumerate(order):
                t0, TT, isctx = cfg.tiles[ti]
                xct, dtile, zt = xc[oi % 2], dtr[oi % 2], zs[oi % 2]
                kb.dma("sp", xct[:, :, :TT], D_["XC"][:, t0:t0 + TT].rearrange("(m r) t -> r m t", r=128),
                       reads=[D_["XC"].d[ti]], writes=xct.d)
                kb.dma("sp", dtile[:, :TT], D_["DTR"][:, t0:t0 + TT], reads=[D_["DTR"].d[ti]], writes=dtile.d)
                kb.dma("sp", zt[:, :, :TT], D_["ZS"][:, :, t0:t0 + TT].rearrange("h r t -> r h t"),
                       reads=[D_["ZS"].d[ti]], writes=zt.d)
                for jb in range(TT // 128 - 1, -1, -1):
                    c = t0 // 128 + jb
                    cs_ = csets[c % 2]
                    dtT, dA, wv, dtw, decc, xT, BT, xw, xdt, xd = (cs_[k_] for k_ in ('dtT', 'dA', 'wv', 'dtw', 'decc', 'xT', 'BT', 'xw', 'xdt', 'xd'))
                    GT, BTm = GTs[c % 2], BTms[c % 2]
                    sl = slice(jb * 128, (jb + 1) * 128)
                    sin = Sin[c % 2]
                    kb.dma("sp", sin[:].rearrange("p h c -> p (h c)"), D_["SINF"][c], reads=[D_["SINF"].d[c]], writes=sin.d)
                    dt_chunk(dtile, jb)
                    tok_major(xct, jb)
                    for d in range(2):
                        kb.tt(xdt[d][:], xT[:], dtT[:, d * 6:d * 6 + 6].unsqueeze(2).broadcast_to([128, 6, 64]), ALU.mult,
                              xT.d + dtT.d, xdt[d].d)
                    kb.tt(xd[:], xT[:], drep.unsqueeze(2).broadcast_to([128, 6, 64]), ALU.mult, xT.d + srep.d, xd.d)
                    for gr in range(2):
                        rs_ = slice(gr * 64, gr * 64 + 64)
                        kb.cp(BTm[rs_, gr, :], xct[rs_, 3, sl], xct.d, BTm.d)
                        kb.mm(pG[:, gr * 128:(gr + 1) * 128], BTm[:, gr, :], xct[:, 4, sl], True, True, BTm.d + xct.d, pG.d)
                    kb.cp(GT[:].rearrange("p g c -> p (g c)"), pG[:, 0:256], pG.d, GT.d, eng="act")
                    for h in range(6):
                        gr = h // 3
                        rs_ = slice(gr * 64, gr * 64 + 64)
                        mts, ces = [], []
                        for d in range(2):
                            dh = d * 6 + h
                            msk, U = (Mgt, Ule) if d == 0 else (Mlt, Uge)
                            bi = 2 * (h % 2) + d
                            co = 256 * (h % 2)
                            ld, dab, e_, ea_, ps_ = Ld[bi], dAb[bi], E[bi], ea[bi], pse[d]
                            mt, ce = MT[nmt % 6], CE[nmt % 6]
                            nmt += 1
                            kb.ts(ld[:], msk, dA[:, dh:dh + 1], ALU.mult, cd + dA.d, ld.d)
                            kb.ts(dab[:], onesf, dA[:, dh:dh + 1], ALU.mult, cd + dA.d, dab.d)
                            kb.mm(ps_[:, co:co + 128], ld[:], U, True, True, ld.d + cd, ps_.d)
                            kb.mm(ps_[:, co + 128:co + 256], dab[:], U, True, True, dab.d + cd, ps_.d)
                            kb.act(e_[:], ps_[:, co:co + 128], AF.Exp, ps_.d, e_.d)
                            kb.act(ea_[:], ps_[:, co + 128:co + 256], AF.Exp, ps_.d, ea_.d)
                            kb.tt(e_[:], e_[:], GT[:, gr, :], ALU.mult, e_.d + GT.d, e_.d)
                            kb.tt(mt[:], e_[:], U, ALU.mult, e_.d + cd, mt.d)
                            kb.tt(ce[:], xct[:, 4, sl], ea_[:], ALU.mult, xct.d + ea_.d, ce.d)
                            if d == 1:
                                kb.cp(decc[:, h:h + 1], ea_[:, 0:1], ea_.d, decc.d)
                            mts.append(mt)
                            ces.append(ce)
                        pyh = py[h // 4][0:64, (h % 4) * 128:(h % 4 + 1) * 128]
                        pd = py[h // 4].d
                        kb.mm(pyh, xdt[0][:, h, :], mts[0][:], True, False, xdt[0].d + mts[0].d, pd)
                        kb.mm(pyh, xdt[1][:, h, :], mts[1][:], False, False, xdt[1].d + mts[1].d, pd)
                        kb.mm(pyh, sin[:, h, :], ces[0][:], False, False, sin.d + ces[0].d, pd)
                        kb.mm(pyh, Sbf[1][:, h, :], ces[1][:], False, False, Sbf[1].d + ces[1].d, pd)
                        kb.mm(pyh, xd[:, h, :], ident, False, True, xd.d + cdb, pd)
                    for hb_ in range(2):
                        nh = 4 if hb_ == 0 else 2
                        kb.tt(yg[:, hb_ * 4:hb_ * 4 + nh, :],
                              py[hb_][0:64, 0:nh * 128].rearrange("p (h t) -> p h t", t=128),
                              zt[:, hb_ * 4:hb_ * 4 + nh, sl], ALU.mult, py[hb_].d + zt.d, yg.d)
                    kb.act(ysq[:], yg[:], AF.Square, yg.d, ysq.d)
                    for h in range(6):
                        kb.mm(pdt[0:64, 128:256], self.cstb[0:64, 0, 0:64], ysq[:, h, :], h == 0, h == 5, cdb + ysq.d, pdt.d)
                    kb.act(rs1[:], pdt[0:64, 128:256], AF.Sqrt, pdt.d, rs1.d, scale=1.0 / BRW, bias=1e-6)
                    kb.recip(rs2[:], rs1[:], rs1.d, rs2.d)
                    for h in range(6):
                        kb.stt(yct[:, h, sl], yg[:, h, :], ng[:, h:h + 1], rs2[:], ALU.mult, ALU.mult,
                               yg.d + ng.d + rs2.d, yct.d)
                    state_update(1, Mlt, decc[:], decc.d)
                kb.dma("pool", D_["YC"][:, :, t0:t0 + TT].rearrange("h r t -> r h t"), yct[:, :, :TT], reads=yct.d,
                       writes=[D_["YC"].d[ti]])
            kb.barrier()

    def gen_mla(self, l, ph):
        kb, cfg, D_ = self.kb, self.cfg, self.dr
        NT, NB = cfg.NT, cfg.NB
        sc = 96.0 ** -0.5
        if True:
            kt = self.sb(ph, "kt", [96, 1, NT], BF16)
            ve = self.sb(ph, "vee", [128, NB, 1, 128], BF16)
            qt = [self.sb(ph, "qt", [96, 1, 512], BF16) for _ in range(2)]
            pT = [self.sb(ph, "pT", [128, 512], BF16) for _ in range(3)]
            dn = self.sb(ph, "dn", [64, 512], F32)
            ob = [self.sb(ph, "ob", [64, 512], BF16) for _ in range(2)]
            pS = [self.ps(ph, "pS", [128, 512], F32) for _ in range(3)]
            po = [self.ps(ph, "po", [128, 512], F32) for _ in range(2)]
            n = 0
            nq = 0
            for hg in range(6):
                for hh in range(1):
                    kb.dma("sp", kt[:, hh, :], D_["KT"][hg + hh], reads=D_["KT"].d, writes=kt.d)
                vsrc = D_["VE"].t.rearrange("b p (h c) -> p b h c", c=128)
                kb.dma("sp", ve[:], vsrc[:, :, hg:hg + 1, :], reads=D_["VE"].d, writes=ve.d)
                steps = []
                for ti, (t0, TT, isctx) in enumerate(cfg.tiles):
                    nkt = 2 if isctx else NB
                    for hh in range(1):
                        for kti in range(nkt):
                            steps.append((ti, hh, kti, nkt))
                qtile = {}

                def get_q(ti):
                    if ti not in qtile:
                        t0, TT, _ = cfg.tiles[ti]
                        q = qt[ti % 2]
                        for hh in range(1):
                            kb.dma("sp", q[:, hh, :TT], D_["QT"][hg + hh, :, t0:t0 + TT], reads=[D_["QT"].d[ti]], writes=q.d)
                        qtile[ti] = q
                    return qtile[ti]

                def qk(si):
                    ti, hh, kti, nkt = steps[si]
                    TT = cfg.tiles[ti][1]
                    q = get_q(ti)
                    ps_ = pS[si % 3]
                    kb.mm(ps_[:, :TT], kt[:, hh, kti * 128:(kti + 1) * 128], q[:, hh, :TT], True, True, kt.d + q.d, ps_.d)

                LA = 2
                for si in range(min(LA, len(steps))):
                    qk(si)
                for si, (ti, hh, kti, nkt) in enumerate(steps):
                    t0, TT, isctx = cfg.tiles[ti]
                    if si + LA < len(steps):
                        qk(si + LA)
                    h = hg + hh
                    acc = po[h % 2]
                    ps_, p_ = pS[si % 3], pT[si % 3]
                    kb.act(p_[:, :TT], ps_[:, :TT], AF.Exp, ps_.d, p_.d, scale=sc)
                    kb.mm(acc[:, :TT], ve[:, kti, hh, :], p_[:, :TT], kti == 0, kti == nkt - 1, ve.d + p_.d, acc.d)
                    if kti == nkt - 1:
                        kb.cp(dn[:, :TT], acc[64:128, :TT], acc.d, dn.d, eng="act")
                        kb.recip(dn[:, :TT], dn[:, :TT], dn.d, dn.d)
                        o = ob[h % 2]
                        kb.tt(o[:, :TT], acc[0:64, :TT], dn[:, :TT], ALU.mult, acc.d + dn.d, o.d)
                        kb.dma("sp", D_["YD"][h, :, t0:t0 + TT], o[:, :TT], reads=o.d, writes=[D_["YD"].d[ti]])
                    yield

    def pass_mla(self, l):
        with ExitStack() as ph:
            for _ in self.gen_mla(l, ph):
                pass
            self.kb.barrier()

    def pass_s5_mla(self, l):
        cfg = self.cfg
        with ExitStack() as ph:
            g1 = self.gen_s5(l, ph)
            next(g1)
            g2 = self.gen_mla(l, ph)
            n1 = 2 * len(cfg.tiles) * 12 + len(cfg.tiles)
            n2 = 6 * (2 + (len(cfg.tiles) - 1) * cfg.NB)
            ratio = max(1, n2 // n1)
            d1 = d2 = False
            while not (d1 and d2):
                if not d1:
                    try:
                        next(g1)
                    except StopIteration:
                        d1 = True
                for _ in range(ratio):
                    if d2:
                        break
                    try:
                        next(g2)
                    except StopIteration:
                        d2 = True
            self.kb.barrier()

    def pass_c(self, l):
        kb, cfg, D_ = self.kb, self.cfg, self.dr
        xsrc = D_["xin"] if l == 0 else D_["out"]
        xv = xsrc.t.rearrange("(k r) t -> r k t", r=128)
        ov = D_["out"].t.rearrange("(k r) t -> r k t", r=128)
        wbr = D_["WBR"][l]
        wout = D_["WOUT"][l].rearrange("(k r) c -> r k c", r=128)
        wfi = D_["WFI"][l]
        wfo = D_["WFO"][l]
        with ExitStack() as ph:
            xs_ = [self.sb(ph, "cx", [128, 8, 512], F32) for _ in range(2)]
            mg = self.sb(ph, "mg", [128, 8, 512], F32)
            mgb = self.sb(ph, "mgb", [128, 8, 512], BF16)
            h2 = self.sb(ph, "h2", [128, 8, 512], BF16)
            ya = self.sb(ph, "cya", [128, 3, 512], BF16)
            yb = self.sb(ph, "cyb", [128, 3, 512], BF16)
            yc = self.sb(ph, "cyc", [64, 6, 512], BF16)
            yd = self.sb(ph, "cyd", [64, 6, 512], BF16)
            gts = [self.sb(ph, "cgt", [128, 8, 512], BF16) for _ in range(2)]
            ngt = 0
            act = self.sb(ph, "cact", [128, 22, 512], BF16)
            wb128 = self.sb(ph, "wb128", [128, 3, 1024], BF16)
            wb64 = self.sb(ph, "wb64", [64, 6, 1024], BF16)
            wo = self.sb(ph, "wo", [128, 8, 1024], BF16)
            wi = [self.sb(ph, "wi", [128, 8, 256], BF16) for _ in range(4)]
            wf = [self.sb(ph, "wf", [128, 22, 128], BF16) for _ in range(3)]
            f = [self.sb(ph, "cf", [128, 512], F32) for _ in range(3)]
            sil = [self.sb(ph, "sil", [128, 512], BF16) for _ in range(2)]
            pm = [self.ps(ph, "cpm", [128, 512], F32) for _ in range(4)]
            pst = self.ps(ph, "cpst", [128, 512], F32)
            npm = 0
            nwi = 0
            nwf = 0
            kb.dma("sp", wo[:], wout, reads=[D_["WOUT"].d[l]], writes=wo.d)
            pre = {}
            st_g = dict(n=0)

            def load_g(ti_, i_):
                t0_, TT_, _ = cfg.tiles[ti_]
                g_ = gts[st_g["n"] % 2]
                st_g["n"] += 1
                kb.dma("sp", g_[:, :, :TT_], D_["GATES"][i_ * D:(i_ + 1) * D, t0_:t0_ + TT_].rearrange("(m r) t -> r m t", r=128),
                       reads=[D_["GATES"].d[ti_]], writes=g_.d)
                return g_

            def load_w(i_):
                if i_ < 2:
                    kb.dma("sp", wb128[:], wbr[i_ * BRW:(i_ + 1) * BRW, :].rearrange("(k r) c -> r k c", r=128),
                           reads=[D_["WBR"].d[l]], writes=wb128.d)
                else:
                    kb.dma("sp", wb64[:], wbr[i_ * BRW:(i_ + 1) * BRW, :].rearrange("(k r) c -> r k c", r=64),
                           reads=[D_["WBR"].d[l]], writes=wb64.d)

            def load_y(ti_):
                t0_, TT_, _ = cfg.tiles[ti_]
                kb.dma("sp", ya[:, :, :TT_], D_["YA"][:, t0_:t0_ + TT_].rearrange("(m r) t -> r m t", r=128), reads=[D_["YA"].d[ti_]], writes=ya.d)
                kb.dma("sp", yb[:, :, :TT_], D_["YB"][:, t0_:t0_ + TT_].rearrange("(m r) t -> r m t", r=128), reads=[D_["YB"].d[ti_]], writes=yb.d)
                kb.dma("sp", yc[:, :, :TT_], D_["YC"][:, :, t0_:t0_ + TT_].rearrange("h r t -> r h t"), reads=[D_["YC"].d[ti_]], writes=yc.d)
                kb.dma("sp", yd[:, :, :TT_], D_["YD"][:, :, t0_:t0_ + TT_].rearrange("h r t -> r h t"), reads=[D_["YD"].d[ti_]], writes=yd.d)

            def load_x(ti_):
                t0_, TT_, _ = cfg.tiles[ti_]
                x_ = xs_[ti_ % 2]
                kb.dma("sp", x_[:, :, :TT_], xv[:, :, t0_:t0_ + TT_], reads=[xsrc.d[ti_] if l else xsrc.d[0]], writes=x_.d)

            load_x(0)
            for ti, (t0, TT, isctx) in enumerate(cfg.tiles):
                j = 1 if isctx else 0
                x = xs_[ti % 2]
                if ti == 0:
                    load_y(0)
                for oi_, i in enumerate((0, 2, 1, 3)):
                    if (ti, i) in pre:
                        gt = pre.pop((ti, i))
                    else:
                        gt = load_g(ti, i)
                        load_w(i)
                    if i < 2:
                        w, y, nk = wb128, (ya, yb)[i], 3
                    else:
                        w, y, nk = wb64, (yc, yd)[i - 2], 6
                    for m in range(8):
                        p = pm[npm % 4]
                        npm += 1
                        for k in range(nk):
                            kb.mm(p[:, :TT], w[:, k, m * 128:(m + 1) * 128], y[:, k, :TT], k == 0, k == nk - 1, w.d + y.d, p.d)
                        if oi_ == 0:
                            kb.tt(mg[:, m, :TT], p[:, :TT], gt[:, m, :TT], ALU.mult, p.d + gt.d, mg.d)
                        else:
                            t_ = f[m % 2]
                            kb.tt(t_[:, :TT], p[:, :TT], gt[:, m, :TT], ALU.mult, p.d + gt.d, t_.d)
                            kb.tt(mg[:, m, :TT], mg[:, m, :TT], t_[:, :TT], ALU.add, mg.d + t_.d, mg.d)
                kb.cp(mgb[:, :, :TT], mg[:, :, :TT], mg.d, mgb.d, eng="act")
                if ti + 1 < len(cfg.tiles):
                    load_y(ti + 1)
                    load_x(ti + 1)
                    for i_ in (0, 2):
                        pre[(ti + 1, i_)] = load_g(ti + 1, i_)
                        load_w(i_)
                for m in range(8):
                    p = pm[npm % 4]
                    npm += 1
                    for k in range(8):
                        kb.mm(p[:, :TT], wo[:, k, m * 128:(m + 1) * 128], mgb[:, k, :TT], k == 0, k == 7, wo.d + mgb.d, p.d)
                    kb.stt(x[:, m, :TT], p[:, :TT], self.MOD[:, 16 + m, j:j + 1], x[:, m, :TT], ALU.mult, ALU.add,
                           p.d + self.MOD.d + x.d, x.d)
                self.norm_mod(x, TT, j, self.S2, 24, mgb, h2, pst, f[0], f[1], xn=mg)
                for sj in range(11):
                    wg, wu = wi[nwi % 4], wi[(nwi + 1) % 4]
                    nwi += 2
                    kb.dma("sp", wg[:], wfi[:, sj * 2048:(sj + 1) * 2048].rearrange("r (k c) -> r k c", c=256), reads=[D_["WFI"].d[l]], writes=wg.d)
                    kb.dma("sp", wu[:], wfi[:, (11 + sj) * 2048:(12 + sj) * 2048].rearrange("r (k c) -> r k c", c=256), reads=[D_["WFI"].d[l]], writes=wu.d)
                    for mi in range(2):
                        pg, pu = pm[npm % 4], pm[(npm + 1) % 4]
                        npm += 2
                        for k in range(8):
                            kb.mm(pg[:, :TT], wg[:, k, mi * 128:(mi + 1) * 128], h2[:, k, :TT], k == 0, k == 7, wg.d + h2.d, pg.d)
                        for k in range(8):
                            kb.mm(pu[:, :TT], wu[:, k, mi * 128:(mi + 1) * 128], h2[:, k, :TT], k == 0, k == 7, wu.d + h2.d, pu.d)
                        sl_ = sil[mi]
                        kb.act(sl_[:, :TT], pg[:, :TT], AF.Silu, pg.d, sl_.d)
                        kb.tt(act[:, 2 * sj + mi, :TT], sl_[:, :TT], pu[:, :TT], ALU.mult, sl_.d + pu.d, act.d)
                for m in range(8):
                    w = wf[nwf % 3]
                    nwf += 1
                    kb.dma("sp", w[:], wfo[:, m * 2816:(m + 1) * 2816].rearrange("r (k c) -> r k c", c=128), reads=[D_["WFO"].d[l]], writes=w.d)
                    p = pm[npm % 4]
                    npm += 1
                    for k in range(22):
                        kb.mm(p[:, :TT], w[:, k, :], act[:, k, :TT], k == 0, k == 21, w.d + act.d, p.d)
                    kb.stt(x[:, m, :TT], p[:, :TT], self.MOD[:, 40 + m, j:j + 1], x[:, m, :TT], ALU.mult, ALU.add,
                           p.d + self.MOD.d + x.d, x.d)
                kb.dma("pool", ov[:, :, t0:t0 + TT], x[:, :, :TT], reads=x.d, writes=[D_["out"].d[ti]])
            kb.barrier()

    def finish(self):
        self.kb.barrier()


def build(cfg, shapes):
    P = Prog(cfg)
    P.declare(shapes)
    P.cast_weights(0)
    P.load_consts()
    for l in range(cfg.depth):
        P.layer_mod(l)
    for l in range(cfg.depth):
        P.set_layer(l)
        P.pass_a(l)
        if l + 1 < cfg.depth:
            P.cast_weights(l + 1)
        if cfg.stop_after == "a":
            break
        if cfg.stop_after in ("s5", "ssd", "ssd1", "mla"):
            P.pass_s5(l)
            if cfg.stop_after == "s5":
                break
            P.pass_ssd(l)
            if cfg.stop_after in ("ssd", "ssd1"):
                break
            P.pass_mla(l)
            break
        P.pass_ssd(l)
        P.pass_s5_mla(l)
        P.pass_c(l)
    P.finish()
    P.es.close()
    return P


def run(inputs, cfg, ncores=2):
    shared = prep_shared(inputs, cfg)
    cores = [prep_core(inputs, b % 2, cfg) for b in range(ncores)]
    shapes = {k: v.shape for k, v in shared.items()}
    shapes.update({k: v.shape for k, v in cores[0].items()})
    P = build(cfg, shapes)
    in_maps = [dict(shared, **c) for c in cores]
    res = run_bass_kernel_spmd(P.nc, in_maps, core_ids=list(range(ncores)))
    return P, res


def kernel(**inputs):
    cfg = Cfg(L=8192, depth=4)
    P, res = run(inputs, cfg, ncores=2)
    out = np.stack([np.ascontiguousarray(res.results[b]["out"][:, CTX:].T) for b in range(2)], axis=0)
    return out.astype(np.float32)
```
